# Optimizing a Trainium2 kernel written in Bass

```python
import jax, jax.numpy as jnp
from jax import lax
import numpy as np

D_MODEL = 1024
BATCH = 8
SEQ = 8192
DEPTH = 1

HEAD_DIM = 64
NA_HEADS = 8
NB_HEADS = 8
NB_KV_HEADS = 2
GRID_W = 64
NA_KH_MAX = 8
NA_KW = 16
WIN = 128
BLOCK = 128
ROPE_THETA = 10000.0
D_FF = 2816
EPS = 1e-6

WIDTH_A = NA_HEADS * HEAD_DIM
WIDTH_BQ = NB_HEADS * HEAD_DIM
WIDTH_BKV = NB_KV_HEADS * HEAD_DIM
D_IN = 3 * WIDTH_A + WIDTH_BQ + 2 * WIDTH_BKV + 2 * D_MODEL
SPLITS = (WIDTH_A, 2 * WIDTH_A, 3 * WIDTH_A,
          3 * WIDTH_A + WIDTH_BQ,
          3 * WIDTH_A + WIDTH_BQ + WIDTH_BKV,
          3 * WIDTH_A + WIDTH_BQ + 2 * WIDTH_BKV,
          3 * WIDTH_A + WIDTH_BQ + 2 * WIDTH_BKV + D_MODEL)

kernel_name = "hybrid_natten_swa_gated_macaron"


def rms_norm(x, g):
    xf = x.astype(jnp.float32)
    y = xf * lax.rsqrt(jnp.mean(xf * xf, axis=-1, keepdims=True) + EPS)
    return (y * g.astype(jnp.float32)).astype(x.dtype)


def swiglu(x, w_gate, w_up, w_down):
    return (jax.nn.silu(x @ w_gate) * (x @ w_up)) @ w_down


def rope(x, pos):
    half = x.shape[-1] // 2
    inv = ROPE_THETA ** (-jnp.arange(half, dtype=jnp.float32) / half)
    ang = pos.astype(jnp.float32)[:, None] * inv[None, :]
    cos = jnp.cos(ang)[None, :, None, :]
    sin = jnp.sin(ang)[None, :, None, :]
    x1 = x[..., :half].astype(jnp.float32)
    x2 = x[..., half:].astype(jnp.float32)
    out = jnp.concatenate([x1 * cos - x2 * sin, x2 * cos + x1 * sin], axis=-1)
    return out.astype(x.dtype)


def neighbourhood_attention_2d(q, k, v, rpb):
    B, T, H, dh = q.shape
    rows = T // GRID_W
    kh = min(NA_KH_MAX, rows)
    q = q.reshape(B, rows, GRID_W, H, dh)
    k = k.reshape(B, rows, GRID_W, H, dh)
    v = v.reshape(B, rows, GRID_W, H, dh)
    cols = jnp.arange(GRID_W)
    col_start = jnp.clip(cols - NA_KW // 2, 0, GRID_W - NA_KW)
    col_idx = col_start[:, None] + jnp.arange(NA_KW)[None, :]
    col_off = col_idx - cols[:, None] + (NA_KW - 1)
    rpb = rpb.astype(jnp.float32)
    scale = dh ** -0.5

    def row_block(args):
        r, q_r = args
        rs = jnp.clip(r - kh // 2, 0, rows - kh)
        k_slab = lax.dynamic_slice_in_dim(k, rs, kh, axis=1)
        v_slab = lax.dynamic_slice_in_dim(v, rs, kh, axis=1)
        k_g = k_slab[:, :, col_idx]
        v_g = v_slab[:, :, col_idx]
        row_off = rs + jnp.arange(kh) - r + (NA_KH_MAX - 1)
        bias = rpb[:, row_off[None, :, None], col_off[:, None, :]]
        s = jnp.einsum('bchd,bicjhd->bhcij', q_r, k_g).astype(jnp.float32) * scale
        s = s + bias[None]
        p = jax.nn.softmax(s.reshape(B, H, GRID_W, kh * NA_KW), axis=-1)
        p = p.reshape(B, H, GRID_W, kh, NA_KW).astype(v.dtype)
        return jnp.einsum('bhcij,bicjhd->bchd', p, v_g)

    out = lax.map(row_block, (jnp.arange(rows), jnp.moveaxis(q, 1, 0)))
    return jnp.moveaxis(out, 0, 1).reshape(B, T, H * dh)


def windowed_gqa_sink(q, k, v, sink):
    B, T, Hq, dh = q.shape
    Hkv = k.shape[2]
    G = Hq // Hkv
    nb = T // BLOCK
    qb = q.reshape(B, nb, BLOCK, Hkv, G, dh)

    def band(a):
        ap = jnp.pad(a, ((0, 0), (BLOCK, BLOCK), (0, 0), (0, 0)))
        ab = ap.reshape(B, nb + 2, BLOCK, Hkv, dh)
        return jnp.concatenate([ab[:, :-2], ab[:, 1:-1], ab[:, 2:]], axis=2)

    kb = band(k)
    vb = band(v)
    qpos = jnp.arange(nb)[:, None] * BLOCK + jnp.arange(BLOCK)[None, :]
    kpos = jnp.arange(nb)[:, None] * BLOCK - BLOCK + jnp.arange(3 * BLOCK)[None, :]
    kp = kpos[:, None, :]
    mask = (jnp.abs(qpos[:, :, None] - kp) <= WIN) & (kp >= 0) & (kp < T)
    s = jnp.einsum('bnqhgd,bnkhd->bnhgqk', qb, kb).astype(jnp.float32) * (dh ** -0.5)
    s = jnp.where(mask[None, :, None, None], s, -jnp.inf)
    sink_l = sink.astype(jnp.float32).reshape(Hkv, G)[None, None, :, :, None, None]
    m = jnp.maximum(jnp.max(s, axis=-1, keepdims=True), sink_l)
    e = jnp.exp(s - m)
    p = e / (jnp.sum(e, axis=-1, keepdims=True) + jnp.exp(sink_l - m))
    o = jnp.einsum('bnhgqk,bnkhd->bnqhgd', p.astype(v.dtype), vb)
    return o.reshape(B, T, Hq * dh)


def setup_inputs(seed: int = 0) -> dict:
    key = jax.random.key(seed)
    ks = jax.random.split(key, 20)
    f32 = jnp.float32

    def w(k, shape, fan_in):
        return jax.random.normal(k, shape, f32) * (fan_in ** -0.5)

    def gain(k, shape):
        return 1.0 + 0.02 * jax.random.normal(k, shape, f32)

    return {
        "x": jax.random.normal(ks[0], (BATCH, SEQ, D_MODEL), f32),
        "ffn1_norm": gain(ks[1], (DEPTH, D_MODEL)),
        "ffn1_w_gate": w(ks[2], (DEPTH, D_MODEL, D_FF), D_MODEL),
        "ffn1_w_up": w(ks[3], (DEPTH, D_MODEL, D_FF), D_MODEL),
        "ffn1_w_down": w(ks[4], (DEPTH, D_FF, D_MODEL), D_FF),
        "mix_norm": gain(ks[5], (DEPTH, D_MODEL)),
        "w_in": w(ks[6], (DEPTH, D_MODEL, D_IN), D_MODEL),
        "na_rpb": 0.1 * jax.random.normal(ks[7], (DEPTH, NA_HEADS, 2 * NA_KH_MAX - 1, 2 * NA_KW - 1), f32),
        "sink_logit": 0.5 * jax.random.normal(ks[8], (DEPTH, NB_HEADS), f32),
        "w_branch_a": w(ks[9], (DEPTH, WIDTH_A, D_MODEL), WIDTH_A),
        "w_branch_b": w(ks[10], (DEPTH, WIDTH_BQ, D_MODEL), WIDTH_BQ),
        "w_out": w(ks[11], (DEPTH, D_MODEL, D_MODEL), D_MODEL),
        "ffn2_norm": gain(ks[12], (DEPTH, D_MODEL)),
        "ffn2_w_gate": w(ks[13], (DEPTH, D_MODEL, D_FF), D_MODEL),
        "ffn2_w_up": w(ks[14], (DEPTH, D_MODEL, D_FF), D_MODEL),
        "ffn2_w_down": w(ks[15], (DEPTH, D_FF, D_MODEL), D_FF),
        "final_norm": gain(ks[16], (D_MODEL,)),
    }


def reference(x, ffn1_norm, ffn1_w_gate, ffn1_w_up, ffn1_w_down, mix_norm, w_in, na_rpb,
              sink_logit, w_branch_a, w_branch_b, w_out, ffn2_norm, ffn2_w_gate, ffn2_w_up,
              ffn2_w_down, final_norm):
    B, T, _ = x.shape
    pos = jnp.arange(T)
    h = x
    for l in range(DEPTH):
        h = h + 0.5 * swiglu(rms_norm(h, ffn1_norm[l]), ffn1_w_gate[l], ffn1_w_up[l], ffn1_w_down[l])
        u = rms_norm(h, mix_norm[l])
        z = u @ w_in[l]
        qa, ka, va, qb, kb, vb, ga, gb = jnp.split(z, SPLITS, axis=-1)
        qa = qa.reshape(B, T, NA_HEADS, HEAD_DIM)
        ka = ka.reshape(B, T, NA_HEADS, HEAD_DIM)
        va = va.reshape(B, T, NA_HEADS, HEAD_DIM)
        ya = neighbourhood_attention_2d(qa, ka, va, na_rpb[l])
        qb = rope(qb.reshape(B, T, NB_HEADS, HEAD_DIM), pos)
        kb = rope(kb.reshape(B, T, NB_KV_HEADS, HEAD_DIM), pos)
        vb = vb.reshape(B, T, NB_KV_HEADS, HEAD_DIM)
        yb = windowed_gqa_sink(qb, kb, vb, sink_logit[l])
        merged = jax.nn.sigmoid(ga) * (ya @ w_branch_a[l]) + jax.nn.sigmoid(gb) * (yb @ w_branch_b[l])
        h = h + merged @ w_out[l]
        h = h + 0.5 * swiglu(rms_norm(h, ffn2_norm[l]), ffn2_w_gate[l], ffn2_w_up[l], ffn2_w_down[l])
    return rms_norm(h, final_norm)
```

```python
import numpy as np
from contextlib import ExitStack
import concourse.bass as bass
import concourse.mybir as mybir
from concourse.bass_utils import run_bass_kernel_spmd

F32 = mybir.dt.float32
BF16 = mybir.dt.bfloat16
AF = mybir.ActivationFunctionType
ALU = mybir.AluOpType

D = 1024
KC = 8
FF = 2816
FC = 22
TB = 512
NT = 4
GRID_W = 64
EPS = 1e-6
NSLOT = 5
SLOT_EL = 2048
RING = 12
NEG = -30000.0

C_QA, C_KA, C_VA, C_QB, C_KB, C_VB, C_GA, C_GB = 0, 512, 1024, 1536, 2048, 2176, 2304, 3328


class _Stop(Exception):
    pass


class Sem:
    def __init__(self, h):
        self.h = h
        self.count = 0


class Buf:
    __slots__ = ("name", "w", "r", "dsem")

    def __init__(self, name):
        self.name = name
        self.w = None
        self.r = {}
        self.dsem = None


class Eng:
    def __init__(self, name, sem, own_raw):
        self.name = name
        self.sem = sem
        self.known = {}
        self.rec = []
        self.own_raw = own_raw


class Tracker:
    def __init__(self, nc, stack):
        self.nc = nc
        self.stack = stack
        self.nsem = 0
        self.engs = {}
        for name, own in (("pe", False), ("act", True), ("dve", True), ("pool", True), ("sp", False)):
            self.engs[name] = Eng(name, self.new_sem(name), own)

    def new_sem(self, name):
        self.nsem += 1
        return Sem(self.stack.enter_context(self.nc.semaphore("s_%s_%d" % (name, self.nsem))))

    def _waits(self, eng, reads, writes):
        need = {}

        def add(tok):
            if tok is None:
                return
            s, v = tok
            if need.get(s, 0) < v:
                need[s] = v

        for b in reads:
            if b.w is not None:
                if b.w[0] is eng.sem and not eng.own_raw:
                    continue
                add(b.w)
        for b in writes:
            if b.w is not None and b.w[0] is not eng.sem:
                add(b.w)
            for s, v in b.r.items():
                if s is not eng.sem:
                    add((s, v))
        out = []
        for s, v in need.items():
            if eng.known.get(s, 0) < v:
                eng.known[s] = v
                out.append((s.h, v))
        return out

    def _update(self, tok, reads, writes):
        for b in writes:
            b.w = tok
            b.r = {}
        for b in reads:
            s, v = tok
            if b.r.get(s, 0) < v:
                b.r[s] = v

    limit = None
    nops = 0
    marks = []

    def mark(self, name):
        self.marks.append((name, self.nops))

    def _count(self):
        if self.limit is not None and self.nops >= self.limit:
            raise _Stop()
        self.nops += 1

    def op(self, engname, fn, reads=(), writes=()):
        self._count()
        eng = self.engs[engname]
        waits = self._waits(eng, reads, writes)
        eng.sem.count += 1
        tok = (eng.sem, eng.sem.count)
        semh = eng.sem.h

        def emit(e, fn=fn, waits=waits, semh=semh):
            for h, v in waits:
                e.wait_ge(h, v)
            fn(e).then_inc(semh, 1)

        eng.rec.append(emit)
        self._update(tok, reads, writes)

    def dma(self, qname, fn, dbuf, reads=(), writes=(), nowait=False):
        self._count()
        eng = self.engs[qname]
        if dbuf.dsem is None:
            dbuf.dsem = self.new_sem("d_" + dbuf.name)
        ds = dbuf.dsem
        waits = [] if nowait else self._waits(eng, reads, writes)
        ds.count += 16
        tok = (ds, ds.count)

        def emit(e, fn=fn, waits=waits, dh=ds.h):
            for h, v in waits:
                e.wait_ge(h, v)
            fn(e).then_inc(dh, 16)

        eng.rec.append(emit)
        self._update(tok, reads, writes)

    def wait_all(self, engname, bufs):
        eng = self.engs[engname]
        waits = self._waits(eng, [], bufs)

        def emit(e, waits=waits):
            for h, v in waits:
                e.wait_ge(h, v)

        eng.rec.append(emit)


def na_plan(T):
    rows = T // GRID_W

    def rs(r):
        return min(max(r - 4, 0), rows - 8)

    variants = {}
    vlist = []
    per_b = []
    for b in range(rows // 2):
        qr = (2 * b, 2 * b + 1)
        lo = min(rs(q) for q in qr)
        hi = max(rs(q) + 7 for q in qr)
        lst = []
        for a in range(lo // 2, hi // 2 + 1):
            valid = tuple(tuple(rs(qr[qh]) <= 2 * a + kh <= rs(qr[qh]) + 7 for qh in range(2)) for kh in range(2))
            key = (2 * a - 2 * b, valid)
            if key not in variants:
                variants[key] = len(vlist)
                vlist.append(key)
            lst.append((a, variants[key]))
        per_b.append(lst)
    return per_b, vlist


def na_table(rpb, vlist):
    nv = len(vlist)
    tab = np.full((2, 64, nv, 8, 2, 64), NEG, dtype=np.float32)
    jj = np.arange(64)[:, None]
    cc = np.arange(64)[None, :]
    cs = np.clip(cc - 8, 0, 48)
    colv = (jj >= cs) & (jj < cs + 16)
    coff = np.clip(jj - cc + 15, 0, 30)
    for vi, (d0, valid) in enumerate(vlist):
        for kh in range(2):
            for qh in range(2):
                if not valid[kh][qh]:
                    continue
                dr = d0 + kh - qh + 7
                g = rpb[:, dr, :][:, coff]
                g = np.where(colv[None], g, np.float32(NEG))
                tab[kh, :, vi, :, qh, :] = np.transpose(g, (1, 0, 2))
    tab = tab[:, :, :, [0, 2, 4, 6, 1, 3, 5, 7]]
    return np.ascontiguousarray(tab.reshape(128, nv, 8, 128))


def rope_tables(T):
    half = 32
    inv = 10000.0 ** (-np.arange(half, dtype=np.float64) / half)
    ang = np.arange(T, dtype=np.float64)[None, :] * inv[:, None]
    c = np.cos(ang).astype(np.float32)
    s = np.sin(ang).astype(np.float32)
    cosT = np.concatenate([c, c, c, c], axis=0)
    sinT = np.concatenate([s, -s, s, -s], axis=0)
    return np.ascontiguousarray(cosT), np.ascontiguousarray(sinT)


def build(T, NV, per_b, stop=None, limit=None):
    NB = T // TB
    Tracker.limit = limit
    Tracker.nops = 0
    Tracker.marks = []

    def ck(n):
        if stop == n:
            raise _Stop()

    nc = bass.Bass("TRN2", target_bir_lowering=False)
    stack = ExitStack()
    tr = Tracker(nc, stack)

    def din(name, shape):
        return nc.dram_tensor(name, list(shape), F32, kind="ExternalInput").ap()

    x = din("x", (T, D))
    w_g1, w_u1, w_d1 = din("w_g1", (D, FF)), din("w_u1", (D, FF)), din("w_d1", (FF, D))
    w_g2, w_u2, w_d2 = din("w_g2", (D, FF)), din("w_u2", (D, FF)), din("w_d2", (FF, D))
    w_in = din("w_in", (D, 4352))
    w_a, w_b, w_o = din("w_a", (512, D)), din("w_b", (512, D)), din("w_o", (D, D))
    gains = din("gains", (128, 24))
    gfin_d = din("gfin", (128, D))
    natab_d = din("natab", (128, NV * 1024))
    sink_d = din("sinkl", (128, 512))
    maskp_d, maskn_d = din("maskp", (128, 1024)), din("maskn", (128, 1024))
    ident_d = din("ident", (128, 128))
    cos_d, sin_d = din("cosT", (128, T)), din("sinT", (128, T))
    y = nc.dram_tensor("y", [T, D], F32, kind="ExternalOutput").ap()

    def dscr(name, shape):
        return nc.dram_tensor(name, list(shape), BF16, kind="Internal").ap()

    sc_gu = [dscr("sc_gu%d" % f, (FC, 128, 2048)) for f in range(2)]
    sc_dn = [dscr("sc_dn%d" % f, (2, 11, 128, 1024)) for f in range(2)]
    sc_k = dscr("sc_k", (6, 128, 1024))
    sc_v = dscr("sc_v", (2, 128, 2048))
    sc_q = dscr("sc_q", (8, 128, 1024))
    sc_ab = dscr("sc_ab", (8, 128, 1024))
    sc_g = dscr("sc_g", (8, 128, 2048))
    sc_o = dscr("sc_o", (2, 4, 128, 1024))

    def sb(name, shape, dt):
        return stack.enter_context(nc.sbuf_tensor("sb_" + name, list(shape), dt))[:]

    hbuf = [sb("h%d" % s, (128, D), F32) for s in range(12)]
    hbuf_b = [Buf("h%d" % s) for s in range(12)]
    hst_b = [Buf("hst%d" % s) for s in range(12)]
    xn_tm = sb("xn_tm", (128, NT, D), BF16)
    xn_tm_b = [Buf("xn_tm%d" % t) for t in range(NT)]
    xnT = [sb("xnT%d" % p, (128, KC, TB), BF16) for p in range(2)]
    xnT_b = [[Buf("xnT%d_%d" % (p, c)) for c in range(KC)] for p in range(2)]
    actT = sb("actT", (128, FC // 2, TB), BF16)
    actT_b = [Buf("actT%d" % j) for j in range(FC // 2)]
    kringA = sb("kringA", (128, 4, RING * 128), BF16)
    kringB = sb("kringB", (128, RING * 128), BF16)
    kring_b = [[Buf("kr%d_%d" % (s, m)) for m in range(5)] for s in range(3)]
    vring = sb("vring", (128, RING, 5, 192), BF16)
    vring_b = [Buf("vr%d" % s) for s in range(RING)]
    vbT = sb("vbT", (128, TB), BF16)
    vbT_b = Buf("vbT")
    qT = sb("qT", (128, 8, TB), BF16)
    qT_b = [Buf("qT%d" % c) for c in range(8)]
    yT = sb("yT", (128, 8, TB), BF16)
    yT_b = [Buf("yT%d" % c) for c in range(8)]
    pbuf = [sb("pbuf%d" % s, (128, 1024), BF16) for s in range(4)]
    pbuf_b = [[Buf("pb%d_%d" % (s, g)) for g in range(2)] for s in range(4)]
    etab = sb("etab", (128, NV, 1024), BF16)
    maskp, maskn = sb("maskp", (128, 1024), BF16), sb("maskn", (128, 1024), BF16)
    ident = sb("ident", (128, 128), BF16)
    const_b = Buf("consts")
    const2_b = Buf("consts2")
    sinke = sb("sinke", (128, 512), F32)
    sinke_b = Buf("sinke")
    gains_s = sb("gains", (128, 24), F32)
    gfin = sb("gfin", (128, D), F32)
    wslot = [sb("wslot%d" % s, (128, SLOT_EL), BF16) for s in range(NSLOT)]
    wslot_b = [Buf("wslot%d" % s) for s in range(NSLOT)]
    rope = [sb("rope%d" % p, (128, 2, TB), F32) for p in range(1)]
    rope_b = [Buf("rope%d" % p) for p in range(1)]
    ftmp = [sb("ftmp%d" % s, (128, TB), F32) for s in range(4)]
    ftmp_b = [Buf("ftmp%d" % s) for s in range(4)]
    stat = [sb("stat%d" % s, (128, 16), F32) for s in range(4)]
    stat_b = [Buf("stat%d" % s) for s in range(4)]
    stage_f = hbuf[0]
    stage_b = hbuf_b[0]

    def ps(name, dt, cols):
        return stack.enter_context(nc.psum_tensor("ps_" + name, [128, cols], dt))[:]

    NPB = 8
    pbank = [ps("pb%d" % s, F32, 512) for s in range(NPB)]
    pbank_b = [Buf("pbank%d" % s) for s in range(NPB)]

    st = {"pb": 0, "tp": 0, "ws": 0, "ft": 0, "stt": 0, "pbf": 0}

    live = set()
    free_banks = list(range(NPB))

    def next_bank():
        if not free_banks:
            raise RuntimeError("out of PSUM banks")
        i = free_banks.pop(0)
        live.add(i)
        return pbank[i], pbank_b[i]

    def rel(*bufs):
        for b in bufs:
            i = pbank_b.index(b)
            if i in live:
                live.discard(i)
                free_banks.append(i)

    def next_ftmp():
        i = st["ft"] % 4
        st["ft"] += 1
        return ftmp[i], ftmp_b[i]

    def next_stat():
        i = st["stt"] % 4
        st["stt"] += 1
        return stat[i], stat_b[i]

    def next_pbuf():
        i = st["pbf"] % 4
        st["pbf"] += 1
        return pbuf[i], pbuf_b[i]

    tr.dma("sp", lambda e: e.dma_start(out=gains_s, in_=gains), const2_b, writes=[const2_b], nowait=True)
    tr.dma("sp", lambda e: e.dma_start(out=gfin, in_=gfin_d), const2_b, writes=[const2_b], nowait=True)
    tr.dma("pool", lambda e: e.dma_start(out=ident, in_=ident_d), const_b, writes=[const_b], nowait=True)
    tr.dma("pool", lambda e: e.dma_start(out=maskp, in_=maskp_d), const_b, writes=[const_b], nowait=True)
    tr.dma("pool", lambda e: e.dma_start(out=maskn, in_=maskn_d), const_b, writes=[const_b], nowait=True)
    for v in range(NV):
        tr.dma("pool", lambda e, v=v: e.dma_start(out=etab[:, v, :], in_=natab_d[:, v * 1024:(v + 1) * 1024]), const_b,
               writes=[const_b], nowait=True)
    tr.dma("sp", lambda e: e.dma_start(out=stage_f[:, 0:512], in_=sink_d), stage_b, writes=[stage_b])
    tr.op("act", lambda e: e.activation(out=sinke, in_=stage_f[:, 0:512], func=AF.Exp), reads=[stage_b], writes=[sinke_b])
    for s in range(RING):
        tr.op("dve", lambda e, s=s: e.memset(vring[:, s, :, 64:128], 1.0), writes=[vring_b[s]])

    grp = {n: Buf("g_" + n) for n in ("f1g0", "f1g1", "f1g2", "f1g3", "f1d", "kv", "q", "m", "o", "f2g0", "f2d")}

    def cv(dst, src, g):
        tr.dma("pool", lambda e, dst=dst, src=src: e.dma_start(out=dst, in_=src), grp[g], writes=[grp[g]], nowait=True)

    def conv_ffn(f, wg, wu, wd, g):
        wgv = wg.rearrange("(k p) n -> p k n", p=128)
        wuv = wu.rearrange("(k p) n -> p k n", p=128)
        for j in range(FC):
            dv = sc_gu[f][j].rearrange("p (w k n) -> p w k n", w=2, k=KC)
            gj = g + "g%d" % (j // 6 if f == 0 else 0)
            cv(dv[:, 0], wgv[:, :, j * 128:(j + 1) * 128], gj)
            cv(dv[:, 1], wuv[:, :, j * 128:(j + 1) * 128], gj)
        wdv = wd.rearrange("(c p) n -> p c n", p=128)
        for hh in range(2):
            for gg in range(11):
                dv = sc_dn[f][hh, gg].rearrange("p (c n) -> p c n", c=2)
                cv(dv, wdv[:, 2 * gg:2 * gg + 2, hh * 512:(hh + 1) * 512], g + "d")

    winv = w_in.rearrange("(k p) n -> p k n", p=128)

    def conv_cols(dst2d, col0, ncols=128, dcol=0, g="kv"):
        dv = dst2d.rearrange("p (k n) -> p k n", k=KC)
        cv(dv[:, :, dcol:dcol + ncols], winv[:, :, col0:col0 + ncols], g)

    conv_ffn(0, w_g1, w_u1, w_d1, "f1")
    for m in range(4):
        conv_cols(sc_k[m], C_KA + m * 128)
    conv_cols(sc_k[4], C_KB)
    conv_cols(sc_k[5], C_VB)
    for u in range(2):
        dv = sc_v[u].rearrange("p (k n) -> p k n", k=4)
        cv(dv, winv[:, 4 * u:4 * u + 4, C_VA:C_VA + 512], "kv")
    for m in range(4):
        conv_cols(sc_q[m], C_QA + m * 128, g="q")
    for c in range(4):
        conv_cols(sc_q[4 + c], C_QB + c * 64, 64, 0, g="q")
        conv_cols(sc_q[4 + c], C_QB + (c + 4) * 64, 64, 64, g="q")
    wav = w_a.rearrange("(m p) n -> p m n", p=128)
    gav = winv
    for j in range(8):
        dv = sc_ab[j].rearrange("p (m n) -> p m n", m=8)
        cv(dv[:, 0:4, :], wav[:, :, j * 128:(j + 1) * 128], "m")
        for c in range(4):
            cv(dv[0:64, 4 + c, :], w_b[c * 64:(c + 1) * 64, j * 128:(j + 1) * 128], "m")
            cv(dv[64:128, 4 + c, :], w_b[(c + 4) * 64:(c + 5) * 64, j * 128:(j + 1) * 128], "m")
        dg = sc_g[j].rearrange("p (w k n) -> p w k n", w=2, k=KC)
        cv(dg[:, 0], gav[:, :, C_GA + j * 128:C_GA + (j + 1) * 128], "m")
        cv(dg[:, 1], gav[:, :, C_GB + j * 128:C_GB + (j + 1) * 128], "m")
    wov = w_o.rearrange("(c p) n -> p c n", p=128)
    for hh in range(2):
        for gg in range(4):
            dv = sc_o[hh, gg].rearrange("p (c n) -> p c n", c=2)
            cv(dv, wov[:, 2 * gg:2 * gg + 2, hh * 512:(hh + 1) * 512], "o")
    conv_ffn(1, w_g2, w_u2, w_d2, "f2")

    def wload(src2d, nel, g, off=0):
        i = st["ws"] % NSLOT
        st["ws"] += 1
        tr.dma("sp", lambda e, i=i, src2d=src2d, nel=nel, off=off: e.dma_start(out=wslot[i][:, 0:nel], in_=src2d[:, off:off + nel]),
               wslot_b[i], reads=[grp[g]], writes=[wslot_b[i]])
        return wslot[i], wslot_b[i]

    def hslot(i, t):
        return (i % 3) * 4 + t

    def rmsnorm_T(i, gcol, dstp, filler=None):
        norm_stats(i)
        if filler is not None:
            filler()
        norm_transposes(gcol, dstp)

    def norm_stats(i):
        sa, sa_b = next_stat()
        tr.op("dve", lambda e: e.memset(sa[:, 0:4], 0.0), writes=[sa_b])
        for t in range(NT):
            hs = hslot(i, t)
            tr.op("act", lambda e, t=t, hs=hs: e.activation(out=xn_tm[:, t, :], in_=hbuf[hs], func=AF.Square,
                                                              accum_out=sa[:, t:t + 1]),
                  reads=[hbuf_b[hs], sa_b], writes=[xn_tm_b[t], sa_b])
        tr.op("dve", lambda e: e.tensor_scalar(out=sa[:, 4:8], in0=sa[:, 0:4], scalar1=1.0 / D, scalar2=EPS,
                                               op0=ALU.mult, op1=ALU.add), reads=[sa_b], writes=[sa_b])
        tr.op("act", lambda e: e.activation(out=sa[:, 8:12], in_=sa[:, 4:8], func=AF.Sqrt), reads=[sa_b], writes=[sa_b])
        tr.op("dve", lambda e: e.reciprocal(out=sa[:, 12:16], in_=sa[:, 8:12]), reads=[sa_b], writes=[sa_b])
        for t in range(NT):
            hs = hslot(i, t)
            tr.op("dve", lambda e, t=t, hs=hs: e.tensor_scalar(out=xn_tm[:, t, :], in0=hbuf[hs],
                                                                 scalar1=sa[:, 12 + t:13 + t], scalar2=None, op0=ALU.mult),
                  reads=[hbuf_b[hs], sa_b], writes=[xn_tm_b[t]])

    def norm_transposes(gcol, dstp):
        for c in range(KC):
            tp, tp_b = next_bank()
            for t in range(NT):
                tr.op("pe", lambda e, tp=tp, t=t, c=c: e.matmul(tp[:, t * 128:(t + 1) * 128],
                                                                lhsT=xn_tm[:, t, c * 128:(c + 1) * 128], rhs=ident,
                                                                start=True, stop=True),
                      reads=[xn_tm_b[t], const_b], writes=[tp_b])
            eng = "act" if c % 2 == 0 else "dve"
            if eng == "act":
                tr.op("act", lambda e, tp=tp, c=c: e.activation(out=xnT[dstp][:, c, :], in_=tp[:, 0:TB], func=AF.Copy,
                                                                scale=gains_s[:, gcol + c:gcol + c + 1]),
                      reads=[tp_b, const2_b], writes=[xnT_b[dstp][c]])
            else:
                tr.op("dve", lambda e, tp=tp, c=c: e.tensor_scalar(out=xnT[dstp][:, c, :], in0=tp[:, 0:TB],
                                                                   scalar1=gains_s[:, gcol + c:gcol + c + 1], scalar2=None,
                                                                   op0=ALU.mult),
                      reads=[tp_b, const2_b], writes=[xnT_b[dstp][c]])
            rel(tp_b)

    def ffn(i, f, p, mid=None, hooks=None):
        g = "f1" if f == 0 else "f2"
        HC = FC // 2
        for half in range(2):
            for jl in range(HC):
                j = half * HC + jl
                if hooks and (half, jl) in hooks:
                    hooks[(half, jl)]()
                ws, ws_b = wload(sc_gu[f][j], 2048, g + "g%d" % (j // 6 if f == 0 else 0))
                pg, pg_b = next_bank()
                pu, pu_b = next_bank()
                for k in range(KC):
                    tr.op("pe", lambda e, k=k, ws=ws, pg=pg: e.matmul(pg, lhsT=ws[:, k * 128:(k + 1) * 128], rhs=xnT[p][:, k, :],
                                                                      start=(k == 0), stop=(k == KC - 1)),
                          reads=[ws_b, xnT_b[p][k]], writes=[pg_b])
                for k in range(KC):
                    tr.op("pe", lambda e, k=k, ws=ws, pu=pu: e.matmul(pu, lhsT=ws[:, 1024 + k * 128:1024 + (k + 1) * 128],
                                                                      rhs=xnT[p][:, k, :], start=(k == 0), stop=(k == KC - 1)),
                          reads=[ws_b, xnT_b[p][k]], writes=[pu_b])
                ft, ft_b = next_ftmp()
                tr.op("act", lambda e, ft=ft, pg=pg: e.activation(out=ft, in_=pg, func=AF.Silu), reads=[pg_b], writes=[ft_b])
                tr.op("dve", lambda e, ft=ft, pu=pu, jl=jl: e.tensor_tensor(out=actT[:, jl, :], in0=ft, in1=pu, op=ALU.mult),
                      reads=[ft_b, pu_b], writes=[actT_b[jl]])
                rel(pg_b, pu_b)
            if half == 1 and mid is not None:
                mid()
            units = [(gg, 0, 2) for gg in range(5)] + [(5, 0, 1)] if half == 0 else [(5, 1, 1)] + [(gg, 0, 2) for gg in range(6, 11)]
            for hh in range(2):
                accs = [next_bank() for _ in range(NT)]
                nmm = 0
                for (gg, c0, ncnk) in units:
                    ws, ws_b = wload(sc_dn[f][hh, gg], 512 * ncnk, g + "d", off=512 * c0)
                    for c2 in range(ncnk):
                        jl = 2 * gg + c0 + c2 - half * HC
                        for t in range(NT):
                            tr.op("pe", lambda e, ws=ws, c2=c2, jl=jl, t=t, acc=accs[t][0], first=(nmm == 0), last=(nmm == HC - 1): e.matmul(
                                acc, lhsT=actT[:, jl, t * 128:(t + 1) * 128], rhs=ws[:, c2 * 512:(c2 + 1) * 512],
                                start=first, stop=last), reads=[ws_b, actT_b[jl]], writes=[accs[t][1]])
                        nmm += 1
                for t in range(NT):
                    hs = hslot(i, t)
                    tr.op("dve", lambda e, hs=hs, hh=hh, acc=accs[t][0]: e.scalar_tensor_tensor(
                        out=hbuf[hs][:, hh * 512:(hh + 1) * 512], in0=acc, scalar=0.5, in1=hbuf[hs][:, hh * 512:(hh + 1) * 512],
                        op0=ALU.mult, op1=ALU.add), reads=[accs[t][1], hbuf_b[hs]], writes=[hbuf_b[hs]])
                rel(*[a_[1] for a_ in accs])

    def proj_fm(ws, ws_b, off, p):
        pb_, pb_b = next_bank()
        for k in range(KC):
            tr.op("pe", lambda e, k=k: e.matmul(pb_, lhsT=ws[:, off + k * 128:off + (k + 1) * 128], rhs=xnT[p][:, k, :],
                                                start=(k == 0), stop=(k == KC - 1)),
                  reads=[ws_b, xnT_b[p][k]], writes=[pb_b])
        return pb_, pb_b

    def rope_evac(pb_, pb_b, rp, dst, dst_b, scl=1.0):
        t1, t1_b = next_ftmp()
        t2, t2_b = next_ftmp()
        cs_, sn_ = rope[rp][:, 0, :], rope[rp][:, 1, :]
        tr.op("dve", lambda e: e.scalar_tensor_tensor(out=t1, in0=pb_, scalar=scl, in1=cs_, op0=ALU.mult, op1=ALU.mult),
              reads=[pb_b, rope_b[rp]], writes=[t1_b])
        for qd in range(4):
            src = (qd ^ 1) * 32
            dstp = qd * 32
            tr.op("dve", lambda e, src=src, dstp=dstp: e.scalar_tensor_tensor(
                out=t2[dstp:dstp + 32, :], in0=pb_[src:src + 32, :], scalar=scl, in1=sn_[src:src + 32, :],
                op0=ALU.mult, op1=ALU.mult), reads=[pb_b, rope_b[rp]], writes=[t2_b])
        tr.op("dve", lambda e: e.tensor_tensor(out=dst, in0=t1, in1=t2, op=ALU.add), reads=[t1_b, t2_b], writes=[dst_b])

    deferred = []

    def load_x(i):
        tok0 = i * TB
        for t in range(NT):
            hs = hslot(i, t)
            tr.dma("sp", lambda e, hs=hs, t=t: e.dma_start(out=hbuf[hs], in_=x[tok0 + t * 128:tok0 + (t + 1) * 128, :]),
                   hbuf_b[hs], writes=[hbuf_b[hs]])

    def stage1(i):
        p = i % 2
        tok0 = i * TB
        tr.dma("sp", lambda e: e.dma_start(out=rope[0][:, 0, :], in_=cos_d[:, tok0:tok0 + TB]), rope_b[0], writes=[rope_b[0]])
        tr.dma("sp", lambda e: e.dma_start(out=rope[0][:, 1, :], in_=sin_d[:, tok0:tok0 + TB]), rope_b[0], writes=[rope_b[0]])
        def run_deferred():
            while deferred:
                deferred.pop(0)()
        ffn(i, 0, p, hooks={(0, 3): run_deferred})
        rmsnorm_T(i, 8, p, filler=(lambda: attn_units(i - 1, True)) if i >= 1 else None)
        rs_ = i % 3
        roff = rs_ * TB
        for u in range(3):
            for c2 in range(2):
                ch = 2 * u + c2
                ws, ws_b = wload(sc_k[ch], 1024, "kv")
                pb_, pb_b = proj_fm(ws, ws_b, 0, p)
                if ch < 4:
                    tr.op("dve", lambda e, ch=ch, pb_=pb_: e.tensor_copy(out=kringA[:, ch, roff:roff + TB], in_=pb_),
                          reads=[pb_b], writes=[kring_b[rs_][ch]])
                elif ch == 4:
                    rope_evac(pb_, pb_b, 0, kringB[:, roff:roff + TB], kring_b[rs_][4])
                else:
                    tr.op("dve", lambda e, pb_=pb_: e.tensor_copy(out=vbT, in_=pb_), reads=[pb_b], writes=[vbT_b])
                rel(pb_b)
        tp, tp_b = next_bank()
        for t in range(NT):
            tr.op("pe", lambda e, t=t: e.matmul(tp[:, t * 128:(t + 1) * 128], lhsT=vbT[:, t * 128:(t + 1) * 128], rhs=ident,
                                                start=True, stop=True),
                  reads=[vbT_b, const_b], writes=[tp_b])
        for t in range(NT):
            s = (4 * i + t) % RING
            tr.op("dve", lambda e, t=t, s=s: e.tensor_copy(out=vring[:, s, 4, 0:64], in_=tp[:, t * 128:t * 128 + 64]),
                  reads=[tp_b], writes=[vring_b[s]])
            tr.op("dve", lambda e, t=t, s=s: e.tensor_copy(out=vring[:, s, 4, 128:192], in_=tp[:, t * 128 + 64:t * 128 + 128]),
                  reads=[tp_b], writes=[vring_b[s]])
        rel(tp_b)
        accs = [next_bank() for _ in range(NT)]
        for u in range(2):
            ws, ws_b = wload(sc_v[u], 2048, "kv")
            for k4 in range(4):
                k = 4 * u + k4
                for t in range(NT):
                    tr.op("pe", lambda e, ws=ws, k4=k4, k=k, t=t, acc=accs[t][0]: e.matmul(
                        acc, lhsT=xnT[p][:, k, t * 128:(t + 1) * 128], rhs=ws[:, k4 * 512:(k4 + 1) * 512],
                        start=(k == 0), stop=(k == KC - 1)), reads=[ws_b, xnT_b[p][k]], writes=[accs[t][1]])
        for t in range(NT):
            s = (4 * i + t) % RING
            av = accs[t][0].rearrange("p (m e d) -> p m e d", m=4, e=2)
            tr.op("dve", lambda e, s=s, av=av: e.tensor_copy(out=vring[:, s, 0:4, 0:64], in_=av[:, :, 0, :]),
                  reads=[accs[t][1]], writes=[vring_b[s]])
            tr.op("dve", lambda e, s=s, av=av: e.tensor_copy(out=vring[:, s, 0:4, 128:192], in_=av[:, :, 1, :]),
                  reads=[accs[t][1]], writes=[vring_b[s]])
        rel(*[a_[1] for a_ in accs])

    def normalize(oe, oe_b, oo, oo_b, ych, qoff, sink):
        r1, r1_b = next_ftmp()
        r2, r2_b = next_ftmp()
        ybs = [yT_b[ych + m] for m in range(4)]
        if sink:
            tr.op("dve", lambda e: e.tensor_tensor(out=r1[0:64, :], in0=oe[64:128, :], in1=sinke[64:128, :], op=ALU.add),
                  reads=[oe_b, sinke_b], writes=[r1_b])
            tr.op("dve", lambda e: e.tensor_tensor(out=r2[64:128, :], in0=oo[0:64, :], in1=sinke[0:64, :], op=ALU.add),
                  reads=[oo_b, sinke_b], writes=[r2_b])
        else:
            tr.op("dve", lambda e: e.tensor_copy(out=r1[0:64, :], in_=oe[64:128, :]), reads=[oe_b], writes=[r1_b])
            tr.op("dve", lambda e: e.tensor_copy(out=r2[64:128, :], in_=oo[0:64, :]), reads=[oo_b], writes=[r2_b])
        ye = yT[0:64, ych:ych + 4, qoff:qoff + 128]
        yo = yT[64:128, ych:ych + 4, qoff:qoff + 128]
        tr.op("dve", lambda e: e.tensor_copy(out=ye, in_=oe[0:64, :].rearrange("p (m q) -> p m q", m=4)),
              reads=[oe_b], writes=ybs)
        tr.op("dve", lambda e: e.tensor_copy(out=yo, in_=oo[64:128, :].rearrange("p (m q) -> p m q", m=4)),
              reads=[oo_b], writes=ybs)

        def deferred_part():
            tr.op("act", lambda e: e.activation(out=r1[0:64, :], in_=r1[0:64, :], func=AF.Ln), reads=[r1_b], writes=[r1_b])
            tr.op("act", lambda e: e.activation(out=r2[64:128, :], in_=r2[64:128, :], func=AF.Ln), reads=[r2_b], writes=[r2_b])
            tr.op("act", lambda e: e.activation(out=r1[0:64, :], in_=r1[0:64, :], func=AF.Exp, scale=-1.0),
                  reads=[r1_b], writes=[r1_b])
            tr.op("act", lambda e: e.activation(out=r2[64:128, :], in_=r2[64:128, :], func=AF.Exp, scale=-1.0),
                  reads=[r2_b], writes=[r2_b])
            tr.op("dve", lambda e: e.tensor_tensor(out=ye, in0=ye, in1=r1[0:64, :].rearrange("p (m q) -> p m q", m=4),
                                                   op=ALU.mult), reads=[r1_b] + ybs, writes=ybs)
            tr.op("dve", lambda e: e.tensor_tensor(out=yo, in0=yo, in1=r2[64:128, :].rearrange("p (m q) -> p m q", m=4),
                                                   op=ALU.mult), reads=[r2_b] + ybs, writes=ybs)
        return deferred_part

    def attention(i, kind, qi, tiles, after_prologue=None):
        qoff = (qi - 4 * i) * 128
        oe, oe_b = next_bank()
        oo, oo_b = next_bank()
        nt = len(tiles)
        steps = []

        def s_step(ti):
            a, var = tiles[ti]
            rsl = (a // 4) % 3
            ktok = (a % RING) * 128
            pb, pb_bs = next_pbuf()
            banks = []
            accs_ = []
            for g in range(2):
                sbk, sbk_b = next_bank()
                banks.append((sbk, sbk_b))
                if kind == "A":
                    bias = etab[:, var, g * 512:(g + 1) * 512]
                elif var != 0:
                    bias = (maskp if var < 0 else maskn)[:, g * 512:(g + 1) * 512]
                else:
                    bias = None
                accs_.append(bias is not None)
                if bias is not None:
                    tr.op("pe", lambda e, sbk=sbk, bias=bias: e.matmul(sbk, lhsT=ident, rhs=bias, start=True, stop=False,
                                                                      skip_group_check=True),
                          reads=[const_b], writes=[sbk_b])
            for g in range(2):
                for hl in range(4):
                    sbk, sbk_b = banks[g]
                    if kind == "A":
                        m, base = hl, g * 64
                        lhsT = kringA[base:base + 64, m, ktok:ktok + 128]
                        rhs = qT[base:base + 64, m, qoff:qoff + 128]
                        rd = [kring_b[rsl][m], qT_b[m]]
                    else:
                        base = g * 64
                        lhsT = kringB[base:base + 64, ktok:ktok + 128]
                        rhs = qT[base:base + 64, 4 + hl, qoff:qoff + 128]
                        rd = [kring_b[rsl][4], qT_b[4 + hl]]
                    tr.op("pe", lambda e, sbk=sbk, hl=hl, lhsT=lhsT, rhs=rhs, acc=accs_[g]: e.matmul(
                        sbk[:, hl * 128:(hl + 1) * 128], lhsT=lhsT, rhs=rhs, start=(not acc), stop=True,
                        skip_group_check=True), reads=rd, writes=[sbk_b])
            for g in range(2):
                sbk, sbk_b = banks[g]
                tr.op("act", lambda e, sbk=sbk, g=g, pb=pb: e.activation(out=pb[:, g * 512:(g + 1) * 512], in_=sbk, func=AF.Exp),
                      reads=[sbk_b], writes=[pb_bs[g]])
            rel(banks[0][1], banks[1][1])
            return pb, pb_bs

        def pv_step(ti, pb, pb_bs):
            a, var = tiles[ti]
            s = a % RING
            if kind == "B":
                for g in range(2):
                    ob, ob_b = (oe, oe_b) if g == 0 else (oo, oo_b)
                    tr.op("pe", lambda e, ob=ob, s=s, g=g, pb=pb: e.matmul(
                        ob, lhsT=vring[:, s, 4, g * 64:g * 64 + 128], rhs=pb[:, g * 512:(g + 1) * 512],
                        start=(ti == 0), stop=(ti == nt - 1), skip_group_check=True),
                          reads=[vring_b[s], pb_bs[g]], writes=[ob_b])
                return
            for h in range(8):
                m, od = h // 2, h % 2
                g = h % 2
                pcol = g * 512 + (h // 2) * 128
                ocol = (h // 2) * 128
                ob, ob_b = (oe, oe_b) if od == 0 else (oo, oo_b)
                first = (ti == 0) and h < 2
                tr.op("pe", lambda e, ob=ob, ocol=ocol, s=s, m=m, od=od, pb=pb, pcol=pcol, first=first: e.matmul(
                    ob[:, ocol:ocol + 128], lhsT=vring[:, s, m, od * 64:od * 64 + 128], rhs=pb[:, pcol:pcol + 128],
                    start=first, stop=(ti == nt - 1), skip_group_check=True),
                      reads=[vring_b[s], pb_bs[g]], writes=[ob_b])

        tr.mark("att%s%d_start" % (kind, qi))
        LOOK = 3
        pend = [s_step(ti) for ti in range(min(LOOK, nt))]
        if after_prologue is not None:
            after_prologue()
        for ti in range(nt):
            if ti + LOOK < nt:
                pend.append(s_step(ti + LOOK))
            pv_step(ti, *pend.pop(0))

        finish = normalize(oe, oe_b, oo, oo_b, 0 if kind == "A" else 4, qoff, kind == "B")
        rel(oe_b, oo_b)
        return finish

    def q_proj(i, chs=range(8)):
        p = i % 2
        for ch in chs:
            ws, ws_b = wload(sc_q[ch], 1024, "q")
            pb_, pb_b = proj_fm(ws, ws_b, 0, p)
            if ch < 4:
                tr.op("act", lambda e, ch=ch, pb_=pb_: e.activation(out=qT[:, ch, :], in_=pb_, func=AF.Copy, scale=0.125),
                      reads=[pb_b], writes=[qT_b[ch]])
            else:
                rope_evac(pb_, pb_b, 0, qT[:, ch, :], qT_b[ch], scl=0.125)
            rel(pb_b)

    def attn_units(i, first):
        nblk = T // 128
        last = (i == NB - 1)
        fin = None
        for b in range(4 * i, 4 * i + 4):
            early = max(a for a, _ in per_b[b]) < 4 * (i + 1)
            if (early and not last) == first:
                fin = attention(i, "A", b, per_b[b], after_prologue=fin)
        for n in range(4 * i, 4 * i + 4):
            early = (n + 1) < 4 * (i + 1)
            if (early and not last) != first:
                continue
            tiles = []
            if n - 1 >= 0:
                tiles.append((n - 1, -1))
            tiles.append((n, 0))
            if n + 1 < nblk:
                tiles.append((n + 1, 1))
            fin = attention(i, "B", n, tiles, after_prologue=fin)
        if fin is not None:
            fin()

    def stage2(i):
        p = i % 2
        tok0 = i * TB
        nxt = i + 2
        if nxt < NB:
            load_x(nxt)
        attn_units(i, False)
        for j in range(8):
            wab, wab_b = wload(sc_ab[j], 1024, "m")
            wg, wg_b = wload(sc_g[j], 2048, "m")
            pa, pa_b = next_bank()
            pbb, pbb_b = next_bank()
            for m in range(4):
                tr.op("pe", lambda e, m=m, wab=wab, pa=pa: e.matmul(pa, lhsT=wab[:, m * 128:(m + 1) * 128], rhs=yT[:, m, :],
                                                                    start=(m == 0), stop=(m == 3)),
                      reads=[wab_b, yT_b[m]], writes=[pa_b])
            for m in range(4):
                tr.op("pe", lambda e, m=m, wab=wab, pbb=pbb: e.matmul(pbb, lhsT=wab[:, 512 + m * 128:512 + (m + 1) * 128],
                                                                      rhs=yT[:, 4 + m, :], start=(m == 0), stop=(m == 3)),
                      reads=[wab_b, yT_b[4 + m]], writes=[pbb_b])
            ga, ga_b = proj_fm(wg, wg_b, 0, p)
            gb, gb_b = proj_fm(wg, wg_b, 1024, p)
            sa_, sa_b = next_ftmp()
            sb_, sb_b = next_ftmp()
            tr.op("act", lambda e, ga=ga, sa_=sa_: e.activation(out=sa_, in_=ga, func=AF.Sigmoid), reads=[ga_b], writes=[sa_b])
            tr.op("act", lambda e, gb=gb, sb_=sb_: e.activation(out=sb_, in_=gb, func=AF.Sigmoid), reads=[gb_b], writes=[sb_b])
            tr.op("dve", lambda e, pa=pa, sa_=sa_: e.tensor_tensor(out=sa_, in0=pa, in1=sa_, op=ALU.mult),
                  reads=[pa_b, sa_b], writes=[sa_b])
            tr.op("dve", lambda e, pbb=pbb, sb_=sb_: e.tensor_tensor(out=sb_, in0=pbb, in1=sb_, op=ALU.mult),
                  reads=[pbb_b, sb_b], writes=[sb_b])
            tr.op("dve", lambda e, j=j, sa_=sa_, sb_=sb_: e.tensor_tensor(out=qT[:, j, :], in0=sa_, in1=sb_, op=ALU.add),
                  reads=[sa_b, sb_b], writes=[qT_b[j]])
            rel(pa_b, pbb_b, ga_b, gb_b)
        for hh in range(2):
            accs = [next_bank() for _ in range(NT)]
            for gg in range(4):
                ws, ws_b = wload(sc_o[hh, gg], 1024, "o")
                for c2 in range(2):
                    k = 2 * gg + c2
                    for t in range(NT):
                        tr.op("pe", lambda e, ws=ws, c2=c2, k=k, t=t, acc=accs[t][0]: e.matmul(
                            acc, lhsT=qT[:, k, t * 128:(t + 1) * 128], rhs=ws[:, c2 * 512:(c2 + 1) * 512],
                            start=(k == 0), stop=(k == KC - 1)), reads=[ws_b, qT_b[k]], writes=[accs[t][1]])
            for t in range(NT):
                hs = hslot(i, t)
                tr.op("dve", lambda e, hs=hs, hh=hh, acc=accs[t][0]: e.tensor_tensor(
                    out=hbuf[hs][:, hh * 512:(hh + 1) * 512], in0=acc, in1=hbuf[hs][:, hh * 512:(hh + 1) * 512], op=ALU.add),
                      reads=[accs[t][1], hbuf_b[hs]], writes=[hbuf_b[hs]])
            rel(*[a_[1] for a_ in accs])
        hooks = {}
        if i + 1 < NB:
            rmsnorm_T(i, 16, p, filler=lambda: q_proj(i + 1, range(0, 4)))
            hooks[(0, 2)] = lambda: q_proj(i + 1, range(4, 8))
        else:
            rmsnorm_T(i, 16, p)
        if nxt < NB:
            hooks[(0, 6)] = lambda: norm_stats(nxt)
            ffn(i, 1, p, mid=lambda: norm_transposes(0, nxt % 2), hooks=hooks)
            deferred.append(lambda: final_norm(i))
        else:
            ffn(i, 1, p, hooks=hooks)
            final_norm(i)

    def final_norm(i):
        tok0 = i * TB
        sa, sa_b = next_stat()
        tr.op("dve", lambda e: e.memset(sa[:, 0:4], 0.0), writes=[sa_b])
        for t in range(NT):
            hs = hslot(i, t)
            tr.op("act", lambda e, t=t, hs=hs: e.activation(out=xn_tm[:, t, :], in_=hbuf[hs], func=AF.Square,
                                                              accum_out=sa[:, t:t + 1]),
                  reads=[hbuf_b[hs], sa_b], writes=[xn_tm_b[t], sa_b])
        tr.op("dve", lambda e: e.tensor_scalar(out=sa[:, 4:8], in0=sa[:, 0:4], scalar1=1.0 / D, scalar2=EPS,
                                               op0=ALU.mult, op1=ALU.add), reads=[sa_b], writes=[sa_b])
        tr.op("act", lambda e: e.activation(out=sa[:, 8:12], in_=sa[:, 4:8], func=AF.Sqrt), reads=[sa_b], writes=[sa_b])
        tr.op("dve", lambda e: e.reciprocal(out=sa[:, 12:16], in_=sa[:, 8:12]), reads=[sa_b], writes=[sa_b])
        for t in range(NT):
            hs = hslot(i, t)
            tr.op("dve", lambda e, t=t, hs=hs: e.scalar_tensor_tensor(out=hbuf[hs], in0=hbuf[hs], scalar=sa[:, 12 + t:13 + t],
                                                                        in1=gfin, op0=ALU.mult, op1=ALU.mult),
                  reads=[hbuf_b[hs], sa_b, const2_b], writes=[hbuf_b[hs]])
            tr.dma("pool", lambda e, hs=hs, t=t: e.dma_start(out=y[tok0 + t * 128:tok0 + (t + 1) * 128, :], in_=hbuf[hs]),
                   hst_b[hs], reads=[hbuf_b[hs]])

    try:
        ck(0)
        load_x(0)
        rmsnorm_T(0, 0, 0)
        for i in range(NB + 1):
            if i < NB:
                stage1(i)
                if i == 0:
                    q_proj(0)
                    if NB > 1:
                        load_x(1)
                        rmsnorm_T(1, 0, 1)
            if i >= 1:
                stage2(i - 1)
        while deferred:
            deferred.pop(0)()
    except _Stop:
        pass
    allb = (hbuf_b + xn_tm_b + sum(xnT_b, []) + actT_b + sum(kring_b, []) + vring_b + [vbT_b] + qT_b + yT_b
            + sum(pbuf_b, []) + [const_b, const2_b, sinke_b] + wslot_b + rope_b + ftmp_b + stat_b
            + pbank_b + list(grp.values()))
    tr.wait_all("sp", allb)

    with nc.Block() as block:
        @block.tensor
        def _(e):
            for f in tr.engs["pe"].rec:
                f(e)

        @block.scalar
        def _(e):
            for f in tr.engs["act"].rec:
                f(e)

        @block.vector
        def _(e):
            for f in tr.engs["dve"].rec:
                f(e)

        @block.gpsimd
        def _(e):
            for f in tr.engs["pool"].rec:
                f(e)

        @block.sync
        def _(e):
            for f in tr.engs["sp"].rec:
                f(e)

    stack.close()
    nc._marks = list(tr.marks)
    return nc


_CACHE = {}


def _consts(T):
    per_b, vlist = na_plan(T)
    kl = np.arange(128)[:, None]
    ql = np.arange(128)[None, :]
    maskp = np.tile(np.where(kl >= ql, np.float32(0.0), np.float32(NEG)), (1, 8)).astype(np.float32)
    maskn = np.tile(np.where(kl <= ql, np.float32(0.0), np.float32(NEG)), (1, 8)).astype(np.float32)
    cosT, sinT = rope_tables(T)
    return per_b, vlist, maskp, maskn, np.eye(128, dtype=np.float32), cosT, sinT


def make_in_maps(inputs, T, nb):
    per_b, vlist, maskp, maskn, ident, cosT, sinT = _consts(T)
    f = lambda a: np.ascontiguousarray(np.asarray(a, dtype=np.float32))
    g1, gm, g2 = (f(inputs[k])[0].reshape(8, 128).T for k in ("ffn1_norm", "mix_norm", "ffn2_norm"))
    gains = np.ascontiguousarray(np.concatenate([g1, gm, g2], axis=1))
    gfin = np.ascontiguousarray(np.broadcast_to(f(inputs["final_norm"])[None, :], (128, D)))
    natab = na_table(f(inputs["na_rpb"])[0], vlist).reshape(128, -1)
    sl = f(inputs["sink_logit"])[0]
    sinkl = np.zeros((128, 4, 128), np.float32)
    sinkl[64:128] = sl[0:4][None, :, None]
    sinkl[0:64] = sl[4:8][None, :, None]
    shared = {
        "w_g1": f(inputs["ffn1_w_gate"])[0], "w_u1": f(inputs["ffn1_w_up"])[0], "w_d1": f(inputs["ffn1_w_down"])[0],
        "w_g2": f(inputs["ffn2_w_gate"])[0], "w_u2": f(inputs["ffn2_w_up"])[0], "w_d2": f(inputs["ffn2_w_down"])[0],
        "w_in": f(inputs["w_in"])[0], "w_a": f(inputs["w_branch_a"])[0], "w_b": f(inputs["w_branch_b"])[0],
        "w_o": f(inputs["w_out"])[0], "gains": gains, "gfin": gfin, "natab": natab,
        "sinkl": sinkl.reshape(128, 512), "maskp": maskp, "maskn": maskn, "ident": ident, "cosT": cosT, "sinT": sinT,
    }
    xs = f(inputs["x"])
    return [dict(shared, x=np.ascontiguousarray(xs[b])) for b in range(nb)], per_b, len(vlist)


def kernel(**inputs):
    x = np.asarray(inputs["x"])
    B, T, _ = x.shape
    in_maps, per_b, nv = make_in_maps(inputs, T, B)
    key = (T, nv)
    if key not in _CACHE:
        _CACHE[key] = build(T, nv, per_b)
    nc = _CACHE[key]
    res = run_bass_kernel_spmd(nc, in_maps, core_ids=list(range(B)))
    return np.stack([np.asarray(r["y"], dtype=np.float32) for r in res.results], axis=0)
```

```python
import numpy as np
from contextlib import ExitStack
import concourse.bass as bass
import concourse.mybir as mybir
from concourse.bass_utils import run_bass_kernel_spmd

F32 = mybir.dt.float32
BF16 = mybir.dt.bfloat16
AF = mybir.ActivationFunctionType
ALU = mybir.AluOpType

D = 1024
KC = 8
FF = 2816
FC = 22
TB = 512
NT = 4
GRID_W = 64
EPS = 1e-6
NSLOT = 5
SLOT_EL = 2048
RING = 12
NEG = -30000.0

C_QA, C_KA, C_VA, C_QB, C_KB, C_VB, C_GA, C_GB = 0, 512, 1024, 1536, 2048, 2176, 2304, 3328


class _Stop(Exception):
    pass


class Sem:
    def __init__(self, h):
        self.h = h
        self.count = 0


class Buf:
    __slots__ = ("name", "w", "r", "dsem")

    def __init__(self, name):
        self.name = name
        self.w = None
        self.r = {}
        self.dsem = None


class Eng:
    def __init__(self, name, sem, own_raw):
        self.name = name
        self.sem = sem
        self.known = {}
        self.rec = []
        self.own_raw = own_raw


class Tracker:
    def __init__(self, nc, stack):
        self.nc = nc
        self.stack = stack
        self.nsem = 0
        self.engs = {}
        for name, own in (("pe", False), ("act", True), ("dve", True), ("pool", True), ("sp", False)):
            self.engs[name] = Eng(name, self.new_sem(name), own)

    def new_sem(self, name):
        self.nsem += 1
        return Sem(self.stack.enter_context(self.nc.semaphore("s_%s_%d" % (name, self.nsem))))

    def _waits(self, eng, reads, writes):
        need = {}

        def add(tok):
            if tok is None:
                return
            s, v = tok
            if need.get(s, 0) < v:
                need[s] = v

        for b in reads:
            if b.w is not None:
                if b.w[0] is eng.sem and not eng.own_raw:
                    continue
                add(b.w)
        for b in writes:
            if b.w is not None and b.w[0] is not eng.sem:
                add(b.w)
            for s, v in b.r.items():
                if s is not eng.sem:
                    add((s, v))
        out = []
        for s, v in need.items():
            if eng.known.get(s, 0) < v:
                eng.known[s] = v
                out.append((s.h, v))
        return out

    def _update(self, tok, reads, writes):
        for b in writes:
            b.w = tok
            b.r = {}
        for b in reads:
            s, v = tok
            if b.r.get(s, 0) < v:
                b.r[s] = v

    limit = None
    nops = 0
    marks = []

    def mark(self, name):
        self.marks.append((name, self.nops))

    def _count(self):
        if self.limit is not None and self.nops >= self.limit:
            raise _Stop()
        self.nops += 1

    def op(self, engname, fn, reads=(), writes=()):
        self._count()
        eng = self.engs[engname]
        waits = self._waits(eng, reads, writes)
        eng.sem.count += 1
        tok = (eng.sem, eng.sem.count)
        semh = eng.sem.h

        def emit(e, fn=fn, waits=waits, semh=semh):
            for h, v in waits:
                e.wait_ge(h, v)
            fn(e).then_inc(semh, 1)

        eng.rec.append(emit)
        self._update(tok, reads, writes)

    def dma(self, qname, fn, dbuf, reads=(), writes=(), nowait=False):
        self._count()
        eng = self.engs[qname]
        if dbuf.dsem is None:
            dbuf.dsem = self.new_sem("d_" + dbuf.name)
        ds = dbuf.dsem
        waits = [] if nowait else self._waits(eng, reads, writes)
        ds.count += 16
        tok = (ds, ds.count)

        def emit(e, fn=fn, waits=waits, dh=ds.h):
            for h, v in waits:
                e.wait_ge(h, v)
            fn(e).then_inc(dh, 16)

        eng.rec.append(emit)
        self._update(tok, reads, writes)

    def wait_all(self, engname, bufs):
        eng = self.engs[engname]
        waits = self._waits(eng, [], bufs)

        def emit(e, waits=waits):
            for h, v in waits:
                e.wait_ge(h, v)

        eng.rec.append(emit)


def na_plan(T):
    rows = T // GRID_W

    def rs(r):
        return min(max(r - 4, 0), rows - 8)

    variants = {}
    vlist = []
    per_b = []
    for b in range(rows // 2):
        qr = (2 * b, 2 * b + 1)
        lo = min(rs(q) for q in qr)
        hi = max(rs(q) + 7 for q in qr)
        lst = []
        for a in range(lo // 2, hi // 2 + 1):
            valid = tuple(tuple(rs(qr[qh]) <= 2 * a + kh <= rs(qr[qh]) + 7 for qh in range(2)) for kh in range(2))
            key = (2 * a - 2 * b, valid)
            if key not in variants:
                variants[key] = len(vlist)
                vlist.append(key)
            lst.append((a, variants[key]))
        per_b.append(lst)
    return per_b, vlist


def na_table(rpb, vlist):
    nv = len(vlist)
    tab = np.full((2, 64, nv, 8, 2, 64), NEG, dtype=np.float32)
    jj = np.arange(64)[:, None]
    cc = np.arange(64)[None, :]
    cs = np.clip(cc - 8, 0, 48)
    colv = (jj >= cs) & (jj < cs + 16)
    coff = np.clip(jj - cc + 15, 0, 30)
    for vi, (d0, valid) in enumerate(vlist):
        for kh in range(2):
            for qh in range(2):
                if not valid[kh][qh]:
                    continue
                dr = d0 + kh - qh + 7
                g = rpb[:, dr, :][:, coff]
                g = np.where(colv[None], g, np.float32(NEG))
                tab[kh, :, vi, :, qh, :] = np.transpose(g, (1, 0, 2))
    tab = tab[:, :, :, [0, 2, 4, 6, 1, 3, 5, 7]]
    return np.ascontiguousarray(tab.reshape(128, nv, 8, 128))


def rope_tables(T):
    half = 32
    inv = 10000.0 ** (-np.arange(half, dtype=np.float64) / half)
    ang = np.arange(T, dtype=np.float64)[None, :] * inv[:, None]
    c = np.cos(ang).astype(np.float32)
    s = np.sin(ang).astype(np.float32)
    cosT = np.concatenate([c, c, c, c], axis=0)
    sinT = np.concatenate([s, -s, s, -s], axis=0)
    return np.ascontiguousarray(cosT), np.ascontiguousarray(sinT)


def build(T, NV, per_b, stop=None, limit=None):
    NB = T // TB
    Tracker.limit = limit
    Tracker.nops = 0
    Tracker.marks = []

    def ck(n):
        if stop == n:
            raise _Stop()

    nc = bass.Bass("TRN2", target_bir_lowering=False)
    stack = ExitStack()
    tr = Tracker(nc, stack)

    def din(name, shape):
        return nc.dram_tensor(name, list(shape), F32, kind="ExternalInput").ap()

    x = din("x", (T, D))
    w_g1, w_u1, w_d1 = din("w_g1", (D, FF)), din("w_u1", (D, FF)), din("w_d1", (FF, D))
    w_g2, w_u2, w_d2 = din("w_g2", (D, FF)), din("w_u2", (D, FF)), din("w_d2", (FF, D))
    w_in = din("w_in", (D, 4352))
    w_a, w_b, w_o = din("w_a", (512, D)), din("w_b", (512, D)), din("w_o", (D, D))
    gains = din("gains", (128, 24))
    gfin_d = din("gfin", (128, D))
    natab_d = din("natab", (128, NV * 1024))
    sink_d = din("sinkl", (128, 512))
    maskp_d, maskn_d = din("maskp", (128, 1024)), din("maskn", (128, 1024))
    ident_d = din("ident", (128, 128))
    cos_d, sin_d = din("cosT", (128, T)), din("sinT", (128, T))
    y = nc.dram_tensor("y", [T, D], F32, kind="ExternalOutput").ap()

    def dscr(name, shape):
        return nc.dram_tensor(name, list(shape), BF16, kind="Internal").ap()

    sc_gu = [dscr("sc_gu%d" % f, (FC, 128, 2048)) for f in range(2)]
    sc_dn = [dscr("sc_dn%d" % f, (2, 11, 128, 1024)) for f in range(2)]
    sc_k = dscr("sc_k", (6, 128, 1024))
    sc_v = dscr("sc_v", (2, 128, 2048))
    sc_q = dscr("sc_q", (8, 128, 1024))
    sc_ab = dscr("sc_ab", (8, 128, 1024))
    sc_g = dscr("sc_g", (8, 128, 2048))
    sc_o = dscr("sc_o", (2, 4, 128, 1024))

    def sb(name, shape, dt):
        return stack.enter_context(nc.sbuf_tensor("sb_" + name, list(shape), dt))[:]

    hbuf = [sb("h%d" % s, (128, D), F32) for s in range(12)]
    hbuf_b = [Buf("h%d" % s) for s in range(12)]
    hst_b = [Buf("hst%d" % s) for s in range(12)]
    xn_tm = sb("xn_tm", (128, NT, D), BF16)
    xn_tm_b = [Buf("xn_tm%d" % t) for t in range(NT)]
    xnT = [sb("xnT%d" % p, (128, KC, TB), BF16) for p in range(2)]
    xnT_b = [[Buf("xnT%d_%d" % (p, c)) for c in range(KC)] for p in range(2)]
    actT = sb("actT", (128, FC // 2, TB), BF16)
    actT_b = [Buf("actT%d" % j) for j in range(FC // 2)]
    kringA = sb("kringA", (128, 4, RING * 128), BF16)
    kringB = sb("kringB", (128, RING * 128), BF16)
    kring_b = [[Buf("kr%d_%d" % (s, m)) for m in range(5)] for s in range(3)]
    vring = sb("vring", (128, RING, 5, 192), BF16)
    vring_b = [Buf("vr%d" % s) for s in range(RING)]
    vbT = sb("vbT", (128, TB), BF16)
    vbT_b = Buf("vbT")
    qT = sb("qT", (128, 8, TB), BF16)
    qT_b = [Buf("qT%d" % c) for c in range(8)]
    yT = sb("yT", (128, 8, TB), BF16)
    yT_b = [Buf("yT%d" % c) for c in range(8)]
    pbuf = [sb("pbuf%d" % s, (128, 1024), BF16) for s in range(4)]
    pbuf_b = [[Buf("pb%d_%d" % (s, g)) for g in range(2)] for s in range(4)]
    etab = sb("etab", (128, NV, 1024), BF16)
    maskp, maskn = sb("maskp", (128, 1024), BF16), sb("maskn", (128, 1024), BF16)
    ident = sb("ident", (128, 128), BF16)
    const_b = Buf("consts")
    const2_b = Buf("consts2")
    sinke = sb("sinke", (128, 512), F32)
    sinke_b = Buf("sinke")
    gains_s = sb("gains", (128, 24), F32)
    gfin = sb("gfin", (128, D), F32)
    wslot = [sb("wslot%d" % s, (128, SLOT_EL), BF16) for s in range(NSLOT)]
    wslot_b = [Buf("wslot%d" % s) for s in range(NSLOT)]
    rope = [sb("rope%d" % p, (128, 2, TB), F32) for p in range(1)]
    rope_b = [Buf("rope%d" % p) for p in range(1)]
    ftmp = [sb("ftmp%d" % s, (128, TB), F32) for s in range(4)]
    ftmp_b = [Buf("ftmp%d" % s) for s in range(4)]
    stat = [sb("stat%d" % s, (128, 16), F32) for s in range(4)]
    stat_b = [Buf("stat%d" % s) for s in range(4)]
    stage_f = hbuf[0]
    stage_b = hbuf_b[0]

    def ps(name, dt, cols):
        return stack.enter_context(nc.psum_tensor("ps_" + name, [128, cols], dt))[:]

    NPB = 8
    pbank = [ps("pb%d" % s, F32, 512) for s in range(NPB)]
    pbank_b = [Buf("pbank%d" % s) for s in range(NPB)]

    st = {"pb": 0, "tp": 0, "ws": 0, "ft": 0, "stt": 0, "pbf": 0}

    live = set()
    free_banks = list(range(NPB))

    def next_bank():
        if not free_banks:
            raise RuntimeError("out of PSUM banks")
        i = free_banks.pop(0)
        live.add(i)
        return pbank[i], pbank_b[i]

    def rel(*bufs):
        for b in bufs:
            i = pbank_b.index(b)
            if i in live:
                live.discard(i)
                free_banks.append(i)

    def next_ftmp():
        i = st["ft"] % 4
        st["ft"] += 1
        return ftmp[i], ftmp_b[i]

    def next_stat():
        i = st["stt"] % 4
        st["stt"] += 1
        return stat[i], stat_b[i]

    def next_pbuf():
        i = st["pbf"] % 4
        st["pbf"] += 1
        return pbuf[i], pbuf_b[i]

    tr.dma("sp", lambda e: e.dma_start(out=gains_s, in_=gains), const2_b, writes=[const2_b], nowait=True)
    tr.dma("sp", lambda e: e.dma_start(out=gfin, in_=gfin_d), const2_b, writes=[const2_b], nowait=True)
    tr.dma("pool", lambda e: e.dma_start(out=ident, in_=ident_d), const_b, writes=[const_b], nowait=True)
    tr.dma("pool", lambda e: e.dma_start(out=maskp, in_=maskp_d), const_b, writes=[const_b], nowait=True)
    tr.dma("pool", lambda e: e.dma_start(out=maskn, in_=maskn_d), const_b, writes=[const_b], nowait=True)
    for v in range(NV):
        tr.dma("pool", lambda e, v=v: e.dma_start(out=etab[:, v, :], in_=natab_d[:, v * 1024:(v + 1) * 1024]), const_b,
               writes=[const_b], nowait=True)
    tr.dma("sp", lambda e: e.dma_start(out=stage_f[:, 0:512], in_=sink_d), stage_b, writes=[stage_b])
    tr.op("act", lambda e: e.activation(out=sinke, in_=stage_f[:, 0:512], func=AF.Exp), reads=[stage_b], writes=[sinke_b])
    for s in range(RING):
        tr.op("dve", lambda e, s=s: e.memset(vring[:, s, :, 64:128], 1.0), writes=[vring_b[s]])

    grp = {n: Buf("g_" + n) for n in ("f1g0", "f1g1", "f1g2", "f1g3", "f1d", "kv", "q", "m", "o", "f2g0", "f2d")}

    def cv(dst, src, g):
        tr.dma("pool", lambda e, dst=dst, src=src: e.dma_start(out=dst, in_=src), grp[g], writes=[grp[g]], nowait=True)

    def conv_ffn(f, wg, wu, wd, g):
        wgv = wg.rearrange("(k p) n -> p k n", p=128)
        wuv = wu.rearrange("(k p) n -> p k n", p=128)
        for j in range(FC):
            dv = sc_gu[f][j].rearrange("p (w k n) -> p w k n", w=2, k=KC)
            gj = g + "g%d" % (j // 6 if f == 0 else 0)
            cv(dv[:, 0], wgv[:, :, j * 128:(j + 1) * 128], gj)
            cv(dv[:, 1], wuv[:, :, j * 128:(j + 1) * 128], gj)
        wdv = wd.rearrange("(c p) n -> p c n", p=128)
        for hh in range(2):
            for gg in range(11):
                dv = sc_dn[f][hh, gg].rearrange("p (c n) -> p c n", c=2)
                cv(dv, wdv[:, 2 * gg:2 * gg + 2, hh * 512:(hh + 1) * 512], g + "d")

    winv = w_in.rearrange("(k p) n -> p k n", p=128)

    def conv_cols(dst2d, col0, ncols=128, dcol=0, g="kv"):
        dv = dst2d.rearrange("p (k n) -> p k n", k=KC)
        cv(dv[:, :, dcol:dcol + ncols], winv[:, :, col0:col0 + ncols], g)

    conv_ffn(0, w_g1, w_u1, w_d1, "f1")
    for m in range(4):
        conv_cols(sc_k[m], C_KA + m * 128)
    conv_cols(sc_k[4], C_KB)
    conv_cols(sc_k[5], C_VB)
    for u in range(2):
        dv = sc_v[u].rearrange("p (k n) -> p k n", k=4)
        cv(dv, winv[:, 4 * u:4 * u + 4, C_VA:C_VA + 512], "kv")
    for m in range(4):
        conv_cols(sc_q[m], C_QA + m * 128, g="q")
    for c in range(4):
        conv_cols(sc_q[4 + c], C_QB + c * 64, 64, 0, g="q")
        conv_cols(sc_q[4 + c], C_QB + (c + 4) * 64, 64, 64, g="q")
    wav = w_a.rearrange("(m p) n -> p m n", p=128)
    gav = winv
    for j in range(8):
        dv = sc_ab[j].rearrange("p (m n) -> p m n", m=8)
        cv(dv[:, 0:4, :], wav[:, :, j * 128:(j + 1) * 128], "m")
        for c in range(4):
            cv(dv[0:64, 4 + c, :], w_b[c * 64:(c + 1) * 64, j * 128:(j + 1) * 128], "m")
            cv(dv[64:128, 4 + c, :], w_b[(c + 4) * 64:(c + 5) * 64, j * 128:(j + 1) * 128], "m")
        dg = sc_g[j].rearrange("p (w k n) -> p w k n", w=2, k=KC)
        cv(dg[:, 0], gav[:, :, C_GA + j * 128:C_GA + (j + 1) * 128], "m")
        cv(dg[:, 1], gav[:, :, C_GB + j * 128:C_GB + (j + 1) * 128], "m")
    wov = w_o.rearrange("(c p) n -> p c n", p=128)
    for hh in range(2):
        for gg in range(4):
            dv = sc_o[hh, gg].rearrange("p (c n) -> p c n", c=2)
            cv(dv, wov[:, 2 * gg:2 * gg + 2, hh * 512:(hh + 1) * 512], "o")
    conv_ffn(1, w_g2, w_u2, w_d2, "f2")

    def wload(src2d, nel, g, off=0):
        i = st["ws"] % NSLOT
        st["ws"] += 1
        tr.dma("sp", lambda e, i=i, src2d=src2d, nel=nel, off=off: e.dma_start(out=wslot[i][:, 0:nel], in_=src2d[:, off:off + nel]),
               wslot_b[i], reads=[grp[g]], writes=[wslot_b[i]])
        return wslot[i], wslot_b[i]

    def hslot(i, t):
        return (i % 3) * 4 + t

    def rmsnorm_T(i, gcol, dstp, filler=None):
        norm_stats(i)
        if filler is not None:
            filler()
        norm_transposes(gcol, dstp)

    def norm_stats(i):
        sa, sa_b = next_stat()
        tr.op("dve", lambda e: e.memset(sa[:, 0:4], 0.0), writes=[sa_b])
        for t in range(NT):
            hs = hslot(i, t)
            tr.op("dve", lambda e, t=t, hs=hs: e.scalar_tensor_tensor(out=xn_tm[:, t, :], in0=hbuf[hs], scalar=1.0, in1=hbuf[hs],
                                                                        op0=ALU.mult, op1=ALU.mult, accum_out=sa[:, t:t + 1]),
                  reads=[hbuf_b[hs], sa_b], writes=[xn_tm_b[t], sa_b])
        tr.op("dve", lambda e: e.tensor_scalar(out=sa[:, 4:8], in0=sa[:, 0:4], scalar1=1.0 / D, scalar2=EPS,
                                               op0=ALU.mult, op1=ALU.add), reads=[sa_b], writes=[sa_b])
        tr.op("act", lambda e: e.activation(out=sa[:, 8:12], in_=sa[:, 4:8], func=AF.Sqrt), reads=[sa_b], writes=[sa_b])
        tr.op("dve", lambda e: e.reciprocal(out=sa[:, 12:16], in_=sa[:, 8:12]), reads=[sa_b], writes=[sa_b])
        for t in range(NT):
            hs = hslot(i, t)
            tr.op("dve", lambda e, t=t, hs=hs: e.tensor_scalar(out=xn_tm[:, t, :], in0=hbuf[hs],
                                                                 scalar1=sa[:, 12 + t:13 + t], scalar2=None, op0=ALU.mult),
                  reads=[hbuf_b[hs], sa_b], writes=[xn_tm_b[t]])

    def norm_transposes(gcol, dstp):
        for c in range(KC):
            tp, tp_b = next_bank()
            for t in range(NT):
                tr.op("pe", lambda e, tp=tp, t=t, c=c: e.matmul(tp[:, t * 128:(t + 1) * 128],
                                                                lhsT=xn_tm[:, t, c * 128:(c + 1) * 128], rhs=ident,
                                                                start=True, stop=True),
                      reads=[xn_tm_b[t], const_b], writes=[tp_b])
            eng = "act" if c % 2 == 0 else "dve"
            if eng == "act":
                tr.op("act", lambda e, tp=tp, c=c: e.activation(out=xnT[dstp][:, c, :], in_=tp[:, 0:TB], func=AF.Copy,
                                                                scale=gains_s[:, gcol + c:gcol + c + 1]),
                      reads=[tp_b, const2_b], writes=[xnT_b[dstp][c]])
            else:
                tr.op("dve", lambda e, tp=tp, c=c: e.tensor_scalar(out=xnT[dstp][:, c, :], in0=tp[:, 0:TB],
                                                                   scalar1=gains_s[:, gcol + c:gcol + c + 1], scalar2=None,
                                                                   op0=ALU.mult),
                      reads=[tp_b, const2_b], writes=[xnT_b[dstp][c]])
            rel(tp_b)

    def ffn(i, f, p, mid=None, hooks=None):
        g = "f1" if f == 0 else "f2"
        HC = FC // 2
        for half in range(2):
            for jl in range(HC):
                j = half * HC + jl
                if hooks and (half, jl) in hooks:
                    hooks[(half, jl)]()
                ws, ws_b = wload(sc_gu[f][j], 2048, g + "g%d" % (j // 6 if f == 0 else 0))
                pg, pg_b = next_bank()
                pu, pu_b = next_bank()
                for k in range(KC):
                    tr.op("pe", lambda e, k=k, ws=ws, pg=pg: e.matmul(pg, lhsT=ws[:, k * 128:(k + 1) * 128], rhs=xnT[p][:, k, :],
                                                                      start=(k == 0), stop=(k == KC - 1)),
                          reads=[ws_b, xnT_b[p][k]], writes=[pg_b])
                for k in range(KC):
                    tr.op("pe", lambda e, k=k, ws=ws, pu=pu: e.matmul(pu, lhsT=ws[:, 1024 + k * 128:1024 + (k + 1) * 128],
                                                                      rhs=xnT[p][:, k, :], start=(k == 0), stop=(k == KC - 1)),
                          reads=[ws_b, xnT_b[p][k]], writes=[pu_b])
                ft, ft_b = next_ftmp()
                tr.op("act", lambda e, ft=ft, pg=pg: e.activation(out=ft, in_=pg, func=AF.Silu), reads=[pg_b], writes=[ft_b])
                tr.op("dve", lambda e, ft=ft, pu=pu, jl=jl: e.tensor_tensor(out=actT[:, jl, :], in0=ft, in1=pu, op=ALU.mult),
                      reads=[ft_b, pu_b], writes=[actT_b[jl]])
                rel(pg_b, pu_b)
            if half == 1 and mid is not None:
                mid()
            units = [(gg, 0, 2) for gg in range(5)] + [(5, 0, 1)] if half == 0 else [(5, 1, 1)] + [(gg, 0, 2) for gg in range(6, 11)]
            for hh in range(2):
                accs = [next_bank() for _ in range(NT)]
                nmm = 0
                for (gg, c0, ncnk) in units:
                    ws, ws_b = wload(sc_dn[f][hh, gg], 512 * ncnk, g + "d", off=512 * c0)
                    for c2 in range(ncnk):
                        jl = 2 * gg + c0 + c2 - half * HC
                        for t in range(NT):
                            tr.op("pe", lambda e, ws=ws, c2=c2, jl=jl, t=t, acc=accs[t][0], first=(nmm == 0), last=(nmm == HC - 1): e.matmul(
                                acc, lhsT=actT[:, jl, t * 128:(t + 1) * 128], rhs=ws[:, c2 * 512:(c2 + 1) * 512],
                                start=first, stop=last), reads=[ws_b, actT_b[jl]], writes=[accs[t][1]])
                        nmm += 1
                for t in range(NT):
                    hs = hslot(i, t)
                    tr.op("dve", lambda e, hs=hs, hh=hh, acc=accs[t][0]: e.scalar_tensor_tensor(
                        out=hbuf[hs][:, hh * 512:(hh + 1) * 512], in0=acc, scalar=0.5, in1=hbuf[hs][:, hh * 512:(hh + 1) * 512],
                        op0=ALU.mult, op1=ALU.add), reads=[accs[t][1], hbuf_b[hs]], writes=[hbuf_b[hs]])
                rel(*[a_[1] for a_ in accs])

    def proj_fm(ws, ws_b, off, p):
        pb_, pb_b = next_bank()
        for k in range(KC):
            tr.op("pe", lambda e, k=k: e.matmul(pb_, lhsT=ws[:, off + k * 128:off + (k + 1) * 128], rhs=xnT[p][:, k, :],
                                                start=(k == 0), stop=(k == KC - 1)),
                  reads=[ws_b, xnT_b[p][k]], writes=[pb_b])
        return pb_, pb_b

    def rope_evac(pb_, pb_b, rp, dst, dst_b, scl=1.0):
        t1, t1_b = next_ftmp()
        t2, t2_b = next_ftmp()
        cs_, sn_ = rope[rp][:, 0, :], rope[rp][:, 1, :]
        tr.op("dve", lambda e: e.scalar_tensor_tensor(out=t1, in0=pb_, scalar=scl, in1=cs_, op0=ALU.mult, op1=ALU.mult),
              reads=[pb_b, rope_b[rp]], writes=[t1_b])
        for qd in range(4):
            src = (qd ^ 1) * 32
            dstp = qd * 32
            tr.op("dve", lambda e, src=src, dstp=dstp: e.scalar_tensor_tensor(
                out=t2[dstp:dstp + 32, :], in0=pb_[src:src + 32, :], scalar=scl, in1=sn_[src:src + 32, :],
                op0=ALU.mult, op1=ALU.mult), reads=[pb_b, rope_b[rp]], writes=[t2_b])
        tr.op("dve", lambda e: e.tensor_tensor(out=dst, in0=t1, in1=t2, op=ALU.add), reads=[t1_b, t2_b], writes=[dst_b])

    deferred = []

    def load_x(i):
        tok0 = i * TB
        for t in range(NT):
            hs = hslot(i, t)
            tr.dma("sp", lambda e, hs=hs, t=t: e.dma_start(out=hbuf[hs], in_=x[tok0 + t * 128:tok0 + (t + 1) * 128, :]),
                   hbuf_b[hs], writes=[hbuf_b[hs]])

    def stage1(i):
        p = i % 2
        tok0 = i * TB
        tr.dma("sp", lambda e: e.dma_start(out=rope[0][:, 0, :], in_=cos_d[:, tok0:tok0 + TB]), rope_b[0], writes=[rope_b[0]])
        tr.dma("sp", lambda e: e.dma_start(out=rope[0][:, 1, :], in_=sin_d[:, tok0:tok0 + TB]), rope_b[0], writes=[rope_b[0]])
        def run_deferred():
            while deferred:
                deferred.pop(0)()
        ffn(i, 0, p, hooks={(0, 3): run_deferred})
        rmsnorm_T(i, 8, p, filler=(lambda: attn_units(i - 1, True)) if i >= 1 else None)
        rs_ = i % 3
        roff = rs_ * TB
        for u in range(3):
            for c2 in range(2):
                ch = 2 * u + c2
                ws, ws_b = wload(sc_k[ch], 1024, "kv")
                pb_, pb_b = proj_fm(ws, ws_b, 0, p)
                if ch < 4:
                    tr.op("dve", lambda e, ch=ch, pb_=pb_: e.tensor_copy(out=kringA[:, ch, roff:roff + TB], in_=pb_),
                          reads=[pb_b], writes=[kring_b[rs_][ch]])
                elif ch == 4:
                    rope_evac(pb_, pb_b, 0, kringB[:, roff:roff + TB], kring_b[rs_][4])
                else:
                    tr.op("dve", lambda e, pb_=pb_: e.tensor_copy(out=vbT, in_=pb_), reads=[pb_b], writes=[vbT_b])
                rel(pb_b)
        tp, tp_b = next_bank()
        for t in range(NT):
            tr.op("pe", lambda e, t=t: e.matmul(tp[:, t * 128:(t + 1) * 128], lhsT=vbT[:, t * 128:(t + 1) * 128], rhs=ident,
                                                start=True, stop=True),
                  reads=[vbT_b, const_b], writes=[tp_b])
        for t in range(NT):
            s = (4 * i + t) % RING
            tr.op("dve", lambda e, t=t, s=s: e.tensor_copy(out=vring[:, s, 4, 0:64], in_=tp[:, t * 128:t * 128 + 64]),
                  reads=[tp_b], writes=[vring_b[s]])
            tr.op("dve", lambda e, t=t, s=s: e.tensor_copy(out=vring[:, s, 4, 128:192], in_=tp[:, t * 128 + 64:t * 128 + 128]),
                  reads=[tp_b], writes=[vring_b[s]])
        rel(tp_b)
        accs = [next_bank() for _ in range(NT)]
        for u in range(2):
            ws, ws_b = wload(sc_v[u], 2048, "kv")
            for k4 in range(4):
                k = 4 * u + k4
                for t in range(NT):
                    tr.op("pe", lambda e, ws=ws, k4=k4, k=k, t=t, acc=accs[t][0]: e.matmul(
                        acc, lhsT=xnT[p][:, k, t * 128:(t + 1) * 128], rhs=ws[:, k4 * 512:(k4 + 1) * 512],
                        start=(k == 0), stop=(k == KC - 1)), reads=[ws_b, xnT_b[p][k]], writes=[accs[t][1]])
        for t in range(NT):
            s = (4 * i + t) % RING
            av = accs[t][0].rearrange("p (m e d) -> p m e d", m=4, e=2)
            tr.op("dve", lambda e, s=s, av=av: e.tensor_copy(out=vring[:, s, 0:4, 0:64], in_=av[:, :, 0, :]),
                  reads=[accs[t][1]], writes=[vring_b[s]])
            tr.op("dve", lambda e, s=s, av=av: e.tensor_copy(out=vring[:, s, 0:4, 128:192], in_=av[:, :, 1, :]),
                  reads=[accs[t][1]], writes=[vring_b[s]])
        rel(*[a_[1] for a_ in accs])

    def normalize(oe, oe_b, oo, oo_b, ych, qoff, sink):
        r1, r1_b = next_ftmp()
        r2, r2_b = next_ftmp()
        ybs = [yT_b[ych + m] for m in range(4)]
        if sink:
            tr.op("dve", lambda e: e.tensor_tensor(out=r1[0:64, :], in0=oe[64:128, :], in1=sinke[64:128, :], op=ALU.add),
                  reads=[oe_b, sinke_b], writes=[r1_b])
            tr.op("dve", lambda e: e.tensor_tensor(out=r2[64:128, :], in0=oo[0:64, :], in1=sinke[0:64, :], op=ALU.add),
                  reads=[oo_b, sinke_b], writes=[r2_b])
        else:
            tr.op("dve", lambda e: e.tensor_copy(out=r1[0:64, :], in_=oe[64:128, :]), reads=[oe_b], writes=[r1_b])
            tr.op("dve", lambda e: e.tensor_copy(out=r2[64:128, :], in_=oo[0:64, :]), reads=[oo_b], writes=[r2_b])
        ye = yT[0:64, ych:ych + 4, qoff:qoff + 128]
        yo = yT[64:128, ych:ych + 4, qoff:qoff + 128]
        tr.op("dve", lambda e: e.tensor_copy(out=ye, in_=oe[0:64, :].rearrange("p (m q) -> p m q", m=4)),
              reads=[oe_b], writes=ybs)
        tr.op("dve", lambda e: e.tensor_copy(out=yo, in_=oo[64:128, :].rearrange("p (m q) -> p m q", m=4)),
              reads=[oo_b], writes=ybs)

        def deferred_part():
            tr.op("act", lambda e: e.activation(out=r1[0:64, :], in_=r1[0:64, :], func=AF.Ln), reads=[r1_b], writes=[r1_b])
            tr.op("act", lambda e: e.activation(out=r2[64:128, :], in_=r2[64:128, :], func=AF.Ln), reads=[r2_b], writes=[r2_b])
            tr.op("act", lambda e: e.activation(out=r1[0:64, :], in_=r1[0:64, :], func=AF.Exp, scale=-1.0),
                  reads=[r1_b], writes=[r1_b])
            tr.op("act", lambda e: e.activation(out=r2[64:128, :], in_=r2[64:128, :], func=AF.Exp, scale=-1.0),
                  reads=[r2_b], writes=[r2_b])
            tr.op("dve", lambda e: e.tensor_tensor(out=ye, in0=ye, in1=r1[0:64, :].rearrange("p (m q) -> p m q", m=4),
                                                   op=ALU.mult), reads=[r1_b] + ybs, writes=ybs)
            tr.op("dve", lambda e: e.tensor_tensor(out=yo, in0=yo, in1=r2[64:128, :].rearrange("p (m q) -> p m q", m=4),
                                                   op=ALU.mult), reads=[r2_b] + ybs, writes=ybs)
        return deferred_part

    def attention(i, kind, qi, tiles, after_prologue=None):
        qoff = (qi - 4 * i) * 128
        oe, oe_b = next_bank()
        oo, oo_b = next_bank()
        nt = len(tiles)
        steps = []

        def s_step(ti):
            a, var = tiles[ti]
            rsl = (a // 4) % 3
            ktok = (a % RING) * 128
            pb, pb_bs = next_pbuf()
            banks = []
            accs_ = []
            for g in range(2):
                sbk, sbk_b = next_bank()
                banks.append((sbk, sbk_b))
                if kind == "A":
                    bias = etab[:, var, g * 512:(g + 1) * 512]
                elif var != 0:
                    bias = (maskp if var < 0 else maskn)[:, g * 512:(g + 1) * 512]
                else:
                    bias = None
                accs_.append(bias is not None)
                if bias is not None:
                    tr.op("pe", lambda e, sbk=sbk, bias=bias: e.matmul(sbk, lhsT=ident, rhs=bias, start=True, stop=False,
                                                                      skip_group_check=True),
                          reads=[const_b], writes=[sbk_b])
            for g in range(2):
                for hl in range(4):
                    sbk, sbk_b = banks[g]
                    if kind == "A":
                        m, base = hl, g * 64
                        lhsT = kringA[base:base + 64, m, ktok:ktok + 128]
                        rhs = qT[base:base + 64, m, qoff:qoff + 128]
                        rd = [kring_b[rsl][m], qT_b[m]]
                    else:
                        base = g * 64
                        lhsT = kringB[base:base + 64, ktok:ktok + 128]
                        rhs = qT[base:base + 64, 4 + hl, qoff:qoff + 128]
                        rd = [kring_b[rsl][4], qT_b[4 + hl]]
                    tr.op("pe", lambda e, sbk=sbk, hl=hl, lhsT=lhsT, rhs=rhs, acc=accs_[g]: e.matmul(
                        sbk[:, hl * 128:(hl + 1) * 128], lhsT=lhsT, rhs=rhs, start=(not acc), stop=True,
                        skip_group_check=True), reads=rd, writes=[sbk_b])
            for g in range(2):
                sbk, sbk_b = banks[g]
                tr.op("act", lambda e, sbk=sbk, g=g, pb=pb: e.activation(out=pb[:, g * 512:(g + 1) * 512], in_=sbk, func=AF.Exp),
                      reads=[sbk_b], writes=[pb_bs[g]])
            rel(banks[0][1], banks[1][1])
            return pb, pb_bs

        def pv_step(ti, pb, pb_bs):
            a, var = tiles[ti]
            s = a % RING
            if kind == "B":
                for g in range(2):
                    ob, ob_b = (oe, oe_b) if g == 0 else (oo, oo_b)
                    tr.op("pe", lambda e, ob=ob, s=s, g=g, pb=pb: e.matmul(
                        ob, lhsT=vring[:, s, 4, g * 64:g * 64 + 128], rhs=pb[:, g * 512:(g + 1) * 512],
                        start=(ti == 0), stop=(ti == nt - 1), skip_group_check=True),
                          reads=[vring_b[s], pb_bs[g]], writes=[ob_b])
                return
            for h in range(8):
                m, od = h // 2, h % 2
                g = h % 2
                pcol = g * 512 + (h // 2) * 128
                ocol = (h // 2) * 128
                ob, ob_b = (oe, oe_b) if od == 0 else (oo, oo_b)
                first = (ti == 0) and h < 2
                tr.op("pe", lambda e, ob=ob, ocol=ocol, s=s, m=m, od=od, pb=pb, pcol=pcol, first=first: e.matmul(
                    ob[:, ocol:ocol + 128], lhsT=vring[:, s, m, od * 64:od * 64 + 128], rhs=pb[:, pcol:pcol + 128],
                    start=first, stop=(ti == nt - 1), skip_group_check=True),
                      reads=[vring_b[s], pb_bs[g]], writes=[ob_b])

        tr.mark("att%s%d_start" % (kind, qi))
        LOOK = 3
        pend = [s_step(ti) for ti in range(min(LOOK, nt))]
        if after_prologue is not None:
            after_prologue()
        for ti in range(nt):
            if ti + LOOK < nt:
                pend.append(s_step(ti + LOOK))
            pv_step(ti, *pend.pop(0))

        finish = normalize(oe, oe_b, oo, oo_b, 0 if kind == "A" else 4, qoff, kind == "B")
        rel(oe_b, oo_b)
        return finish

    def q_proj(i, chs=range(8)):
        p = i % 2
        for ch in chs:
            ws, ws_b = wload(sc_q[ch], 1024, "q")
            pb_, pb_b = proj_fm(ws, ws_b, 0, p)
            if ch < 4:
                tr.op("act", lambda e, ch=ch, pb_=pb_: e.activation(out=qT[:, ch, :], in_=pb_, func=AF.Copy, scale=0.125),
                      reads=[pb_b], writes=[qT_b[ch]])
            else:
                rope_evac(pb_, pb_b, 0, qT[:, ch, :], qT_b[ch], scl=0.125)
            rel(pb_b)

    def attn_units(i, first):
        nblk = T // 128
        last = (i == NB - 1)
        fin = None
        for b in range(4 * i, 4 * i + 4):
            early = max(a for a, _ in per_b[b]) < 4 * (i + 1)
            if (early and not last) == first:
                fin = attention(i, "A", b, per_b[b], after_prologue=fin)
        for n in range(4 * i, 4 * i + 4):
            early = (n + 1) < 4 * (i + 1)
            if (early and not last) != first:
                continue
            tiles = []
            if n - 1 >= 0:
                tiles.append((n - 1, -1))
            tiles.append((n, 0))
            if n + 1 < nblk:
                tiles.append((n + 1, 1))
            fin = attention(i, "B", n, tiles, after_prologue=fin)
        if fin is not None:
            fin()

    def stage2(i):
        p = i % 2
        tok0 = i * TB
        nxt = i + 2
        if nxt < NB:
            load_x(nxt)
        attn_units(i, False)
        for j in range(8):
            wab, wab_b = wload(sc_ab[j], 1024, "m")
            wg, wg_b = wload(sc_g[j], 2048, "m")
            pa, pa_b = next_bank()
            pbb, pbb_b = next_bank()
            for m in range(4):
                tr.op("pe", lambda e, m=m, wab=wab, pa=pa: e.matmul(pa, lhsT=wab[:, m * 128:(m + 1) * 128], rhs=yT[:, m, :],
                                                                    start=(m == 0), stop=(m == 3)),
                      reads=[wab_b, yT_b[m]], writes=[pa_b])
            for m in range(4):
                tr.op("pe", lambda e, m=m, wab=wab, pbb=pbb: e.matmul(pbb, lhsT=wab[:, 512 + m * 128:512 + (m + 1) * 128],
                                                                      rhs=yT[:, 4 + m, :], start=(m == 0), stop=(m == 3)),
                      reads=[wab_b, yT_b[4 + m]], writes=[pbb_b])
            ga, ga_b = proj_fm(wg, wg_b, 0, p)
            gb, gb_b = proj_fm(wg, wg_b, 1024, p)
            sa_, sa_b = next_ftmp()
            sb_, sb_b = next_ftmp()
            tr.op("act", lambda e, ga=ga, sa_=sa_: e.activation(out=sa_, in_=ga, func=AF.Sigmoid), reads=[ga_b], writes=[sa_b])
            tr.op("act", lambda e, gb=gb, sb_=sb_: e.activation(out=sb_, in_=gb, func=AF.Sigmoid), reads=[gb_b], writes=[sb_b])
            tr.op("dve", lambda e, pa=pa, sa_=sa_: e.tensor_tensor(out=sa_, in0=pa, in1=sa_, op=ALU.mult),
                  reads=[pa_b, sa_b], writes=[sa_b])
            tr.op("dve", lambda e, pbb=pbb, sb_=sb_: e.tensor_tensor(out=sb_, in0=pbb, in1=sb_, op=ALU.mult),
                  reads=[pbb_b, sb_b], writes=[sb_b])
            tr.op("dve", lambda e, j=j, sa_=sa_, sb_=sb_: e.tensor_tensor(out=qT[:, j, :], in0=sa_, in1=sb_, op=ALU.add),
                  reads=[sa_b, sb_b], writes=[qT_b[j]])
            rel(pa_b, pbb_b, ga_b, gb_b)
        for hh in range(2):
            accs = [next_bank() for _ in range(NT)]
            for gg in range(4):
                ws, ws_b = wload(sc_o[hh, gg], 1024, "o")
                for c2 in range(2):
                    k = 2 * gg + c2
                    for t in range(NT):
                        tr.op("pe", lambda e, ws=ws, c2=c2, k=k, t=t, acc=accs[t][0]: e.matmul(
                            acc, lhsT=qT[:, k, t * 128:(t + 1) * 128], rhs=ws[:, c2 * 512:(c2 + 1) * 512],
                            start=(k == 0), stop=(k == KC - 1)), reads=[ws_b, qT_b[k]], writes=[accs[t][1]])
            for t in range(NT):
                hs = hslot(i, t)
                tr.op("dve", lambda e, hs=hs, hh=hh, acc=accs[t][0]: e.tensor_tensor(
                    out=hbuf[hs][:, hh * 512:(hh + 1) * 512], in0=acc, in1=hbuf[hs][:, hh * 512:(hh + 1) * 512], op=ALU.add),
                      reads=[accs[t][1], hbuf_b[hs]], writes=[hbuf_b[hs]])
            rel(*[a_[1] for a_ in accs])
        hooks = {}
        if i + 1 < NB:
            rmsnorm_T(i, 16, p, filler=lambda: q_proj(i + 1, range(0, 4)))
            hooks[(0, 2)] = lambda: q_proj(i + 1, range(4, 8))
        else:
            rmsnorm_T(i, 16, p)
        if nxt < NB:
            hooks[(0, 6)] = lambda: norm_stats(nxt)
            ffn(i, 1, p, mid=lambda: norm_transposes(0, nxt % 2), hooks=hooks)
            deferred.append(lambda: final_norm(i))
        else:
            ffn(i, 1, p, hooks=hooks)
            final_norm(i)

    def final_norm(i):
        tok0 = i * TB
        sa, sa_b = next_stat()
        tr.op("dve", lambda e: e.memset(sa[:, 0:4], 0.0), writes=[sa_b])
        for t in range(NT):
            hs = hslot(i, t)
            tr.op("dve", lambda e, t=t, hs=hs: e.scalar_tensor_tensor(out=xn_tm[:, t, :], in0=hbuf[hs], scalar=1.0, in1=hbuf[hs],
                                                                        op0=ALU.mult, op1=ALU.mult, accum_out=sa[:, t:t + 1]),
                  reads=[hbuf_b[hs], sa_b], writes=[xn_tm_b[t], sa_b])
        tr.op("dve", lambda e: e.tensor_scalar(out=sa[:, 4:8], in0=sa[:, 0:4], scalar1=1.0 / D, scalar2=EPS,
                                               op0=ALU.mult, op1=ALU.add), reads=[sa_b], writes=[sa_b])
        tr.op("act", lambda e: e.activation(out=sa[:, 8:12], in_=sa[:, 4:8], func=AF.Sqrt), reads=[sa_b], writes=[sa_b])
        tr.op("dve", lambda e: e.reciprocal(out=sa[:, 12:16], in_=sa[:, 8:12]), reads=[sa_b], writes=[sa_b])
        for t in range(NT):
            hs = hslot(i, t)
            tr.op("dve", lambda e, t=t, hs=hs: e.scalar_tensor_tensor(out=hbuf[hs], in0=hbuf[hs], scalar=sa[:, 12 + t:13 + t],
                                                                        in1=gfin, op0=ALU.mult, op1=ALU.mult),
                  reads=[hbuf_b[hs], sa_b, const2_b], writes=[hbuf_b[hs]])
            tr.dma("pool", lambda e, hs=hs, t=t: e.dma_start(out=y[tok0 + t * 128:tok0 + (t + 1) * 128, :], in_=hbuf[hs]),
                   hst_b[hs], reads=[hbuf_b[hs]])

    try:
        ck(0)
        load_x(0)
        rmsnorm_T(0, 0, 0)
        for i in range(NB + 1):
            if i < NB:
                stage1(i)
                if i == 0:
                    q_proj(0)
                    if NB > 1:
                        load_x(1)
                        rmsnorm_T(1, 0, 1)
            if i >= 1:
                stage2(i - 1)
        while deferred:
            deferred.pop(0)()
    except _Stop:
        pass
    allb = (hbuf_b + xn_tm_b + sum(xnT_b, []) + actT_b + sum(kring_b, []) + vring_b + [vbT_b] + qT_b + yT_b
            + sum(pbuf_b, []) + [const_b, const2_b, sinke_b] + wslot_b + rope_b + ftmp_b + stat_b
            + pbank_b + list(grp.values()))
    tr.wait_all("sp", allb)

    with nc.Block() as block:
        @block.tensor
        def _(e):
            for f in tr.engs["pe"].rec:
                f(e)

        @block.scalar
        def _(e):
            for f in tr.engs["act"].rec:
                f(e)

        @block.vector
        def _(e):
            for f in tr.engs["dve"].rec:
                f(e)

        @block.gpsimd
        def _(e):
            for f in tr.engs["pool"].rec:
                f(e)

        @block.sync
        def _(e):
            for f in tr.engs["sp"].rec:
                f(e)

    stack.close()
    nc._marks = list(tr.marks)
    return nc


_CACHE = {}


def _consts(T):
    per_b, vlist = na_plan(T)
    kl = np.arange(128)[:, None]
    ql = np.arange(128)[None, :]
    maskp = np.tile(np.where(kl >= ql, np.float32(0.0), np.float32(NEG)), (1, 8)).astype(np.float32)
    maskn = np.tile(np.where(kl <= ql, np.float32(0.0), np.float32(NEG)), (1, 8)).astype(np.float32)
    cosT, sinT = rope_tables(T)
    return per_b, vlist, maskp, maskn, np.eye(128, dtype=np.float32), cosT, sinT


def make_in_maps(inputs, T, nb):
    per_b, vlist, maskp, maskn, ident, cosT, sinT = _consts(T)
    f = lambda a: np.ascontiguousarray(np.asarray(a, dtype=np.float32))
    g1, gm, g2 = (f(inputs[k])[0].reshape(8, 128).T for k in ("ffn1_norm", "mix_norm", "ffn2_norm"))
    gains = np.ascontiguousarray(np.concatenate([g1, gm, g2], axis=1))
    gfin = np.ascontiguousarray(np.broadcast_to(f(inputs["final_norm"])[None, :], (128, D)))
    natab = na_table(f(inputs["na_rpb"])[0], vlist).reshape(128, -1)
    sl = f(inputs["sink_logit"])[0]
    sinkl = np.zeros((128, 4, 128), np.float32)
    sinkl[64:128] = sl[0:4][None, :, None]
    sinkl[0:64] = sl[4:8][None, :, None]
    shared = {
        "w_g1": f(inputs["ffn1_w_gate"])[0], "w_u1": f(inputs["ffn1_w_up"])[0], "w_d1": f(inputs["ffn1_w_down"])[0],
        "w_g2": f(inputs["ffn2_w_gate"])[0], "w_u2": f(inputs["ffn2_w_up"])[0], "w_d2": f(inputs["ffn2_w_down"])[0],
        "w_in": f(inputs["w_in"])[0], "w_a": f(inputs["w_branch_a"])[0], "w_b": f(inputs["w_branch_b"])[0],
        "w_o": f(inputs["w_out"])[0], "gains": gains, "gfin": gfin, "natab": natab,
        "sinkl": sinkl.reshape(128, 512), "maskp": maskp, "maskn": maskn, "ident": ident, "cosT": cosT, "sinT": sinT,
    }
    xs = f(inputs["x"])
    return [dict(shared, x=np.ascontiguousarray(xs[b])) for b in range(nb)], per_b, len(vlist)


def kernel(**inputs):
    x = np.asarray(inputs["x"])
    B, T, _ = x.shape
    in_maps, per_b, nv = make_in_maps(inputs, T, B)
    key = (T, nv)
    if key not in _CACHE:
        _CACHE[key] = build(T, nv, per_b)
    nc = _CACHE[key]
    res = run_bass_kernel_spmd(nc, in_maps, core_ids=list(range(B)))
    return np.stack([np.asarray(r["y"], dtype=np.float32) for r in res.results], axis=0)
```
